# Optimizing a Trainium2 kernel written in Bass

```python
import jax, jax.numpy as jnp
from jax import lax
import numpy as np

D_MODEL = 2048
BATCH = 16
SEQ = 256
DEPTH = 2
DEC_BATCH = 4
DEC_SEQ = 4096
PAST_LEN = 512

GRID_W = 64
N_HEADS = 16
HEAD_DIM = 64
D_NA = N_HEADS * HEAD_DIM
D_CONV = 1024
CONV_WIDTH = 31
WIN_ROWS = 8
WIN_COLS = 16
D_FF = 5504
N_MOD = 9
FFN_RES = 0.5
EPS = 1e-6
IN_SPLITS = (2 * D_CONV, 2 * D_CONV + D_NA, 2 * D_CONV + 2 * D_NA,
             2 * D_CONV + 3 * D_NA, 2 * D_CONV + 3 * D_NA + D_MODEL)
IN_WIDTH = 2 * D_CONV + 3 * D_NA + 2 * D_MODEL

kernel_name = "hybrid_conv_natten_diffusion_step"


def rmsnorm(x, g):
    x32 = x.astype(jnp.float32)
    y = x32 * lax.rsqrt(jnp.mean(x32 * x32, axis=-1, keepdims=True) + EPS)
    return y.astype(x.dtype) * g


def layernorm(x, g, b):
    x32 = x.astype(jnp.float32)
    mu = jnp.mean(x32, axis=-1, keepdims=True)
    var = jnp.mean(jnp.square(x32 - mu), axis=-1, keepdims=True)
    y = (x32 - mu) * lax.rsqrt(var + EPS)
    return y.astype(x.dtype) * g + b


def modulate(x, shift, scale):
    return x * (1 + scale) + shift


def ffn_sublayer(x, shift, scale, gate, g, w_gate, w_up, w_down):
    h = modulate(rmsnorm(x, g), shift, scale)
    return x + FFN_RES * gate * ((jax.nn.silu(h @ w_gate) * (h @ w_up)) @ w_down)


def in_projection(u, w_in):
    z = u @ w_in
    conv_in, q, k, v, g_conv, g_na = jnp.split(z, IN_SPLITS, axis=-1)
    b, t = u.shape[0], u.shape[1]
    q = q.reshape(b, t, N_HEADS, HEAD_DIM)
    k = k.reshape(b, t, N_HEADS, HEAD_DIM)
    v = v.reshape(b, t, N_HEADS, HEAD_DIM)
    return conv_in, q, k, v, g_conv, g_na


def conv_branch(conv_in, w_dw, b_dw, ln_g, ln_b):
    a, g = jnp.split(conv_in, 2, axis=-1)
    h = a * jax.nn.sigmoid(g)
    pad = CONV_WIDTH // 2
    h = lax.conv_general_dilated(h, w_dw[:, None, :], window_strides=(1,),
                                 padding=[(pad, pad)],
                                 dimension_numbers=('NWC', 'WIO', 'NWC'),
                                 feature_group_count=D_CONV) + b_dw
    return jax.nn.silu(layernorm(h, ln_g, ln_b))


def context_attention(q, k, v):
    b, t = q.shape[0], q.shape[1]
    s = jnp.einsum('bqhd,bkhd->bhqk', q, k).astype(jnp.float32) * (HEAD_DIM ** -0.5)
    p = jax.nn.softmax(s, axis=-1).astype(q.dtype)
    return jnp.einsum('bhqk,bkhd->bqhd', p, v).reshape(b, t, D_NA)


def neighbourhood_attention(q, k, v, ck, cv, rel_bias):
    b, t, h, dh = q.shape
    rows = t // GRID_W
    kr = min(WIN_ROWS, rows)
    kc = WIN_COLS
    qg = q.reshape(b, rows, GRID_W, h, dh)
    kg = k.reshape(b, rows, GRID_W, h, dh)
    vg = v.reshape(b, rows, GRID_W, h, dh)
    col = jnp.arange(GRID_W)
    col_start = jnp.clip(col - kc // 2, 0, GRID_W - kc)
    col_idx = col_start[:, None] + jnp.arange(kc)[None, :]
    dcol = col_idx - col[:, None] + (WIN_COLS - 1)
    scale = HEAD_DIM ** -0.5
    n_loc = kr * kc

    def row_block(r):
        r_start = jnp.clip(r - kr // 2, 0, rows - kr)
        q_r = lax.dynamic_index_in_dim(qg, r, axis=1, keepdims=False)
        k_band = lax.dynamic_slice_in_dim(kg, r_start, kr, axis=1)
        v_band = lax.dynamic_slice_in_dim(vg, r_start, kr, axis=1)
        k_win = k_band[:, :, col_idx]
        v_win = v_band[:, :, col_idx]
        drow = r_start + jnp.arange(kr) - r + (WIN_ROWS - 1)
        bias = rel_bias[:, drow[:, None, None], dcol[None, :, :]]
        bias = jnp.transpose(bias, (0, 2, 1, 3))
        s_loc = jnp.einsum('bwhd,biwjhd->bhwij', q_r, k_win).astype(jnp.float32) * scale
        s_loc = (s_loc + bias[None].astype(jnp.float32)).reshape(b, h, GRID_W, n_loc)
        s_ctx = jnp.einsum('bwhd,blhd->bhwl', q_r, ck).astype(jnp.float32) * scale
        p = jax.nn.softmax(jnp.concatenate([s_loc, s_ctx], axis=-1), axis=-1).astype(q.dtype)
        p_loc = p[..., :n_loc].reshape(b, h, GRID_W, kr, kc)
        p_ctx = p[..., n_loc:]
        return (jnp.einsum('bhwij,biwjhd->bwhd', p_loc, v_win)
                + jnp.einsum('bhwl,blhd->bwhd', p_ctx, cv))

    out = lax.map(row_block, jnp.arange(rows))
    return jnp.transpose(out, (1, 0, 2, 3, 4)).reshape(b, t, D_NA)


def merge_branches(conv_h, na_o, g_conv, g_na, w_conv_out, w_na_out, w_out):
    m = (jax.nn.sigmoid(g_conv) * (conv_h @ w_conv_out)
         + jax.nn.sigmoid(g_na) * (na_o @ w_na_out))
    return m @ w_out


def setup_inputs(seed: int = 0) -> dict:
    key = jax.random.key(seed)
    ks = jax.random.split(key, 24)
    D = D_MODEL

    def nrm(k, shape, s):
        return jax.random.normal(k, shape, jnp.float32) * s

    return {
        "x_prompt": nrm(ks[0], (BATCH, SEQ, D), 1.0),
        "x_sample": nrm(ks[1], (DEC_BATCH, DEC_SEQ, D), 1.0),
        "cache_k": nrm(ks[2], (DEC_BATCH, DEPTH, PAST_LEN, N_HEADS, HEAD_DIM), 1.0),
        "cache_v": nrm(ks[3], (DEC_BATCH, DEPTH, PAST_LEN, N_HEADS, HEAD_DIM), 1.0),
        "c": nrm(ks[4], (DEC_BATCH, D), 1.0),
        "c_ctx": nrm(ks[5], (D,), 1.0),
        "w_ada": nrm(ks[6], (DEPTH, D, N_MOD * D), 0.5 * D ** -0.5),
        "b_ada": nrm(ks[7], (DEPTH, N_MOD * D), 0.01),
        "norm_g": 1.0 + nrm(ks[8], (DEPTH, 3, D), 0.02),
        "ffn_w_gate": nrm(ks[9], (DEPTH, 2, D, D_FF), D ** -0.5),
        "ffn_w_up": nrm(ks[10], (DEPTH, 2, D, D_FF), D ** -0.5),
        "ffn_w_down": nrm(ks[11], (DEPTH, 2, D_FF, D), D_FF ** -0.5),
        "w_in": nrm(ks[12], (DEPTH, D, IN_WIDTH), D ** -0.5),
        "conv_w": nrm(ks[13], (DEPTH, CONV_WIDTH, D_CONV), CONV_WIDTH ** -0.5),
        "conv_b": nrm(ks[14], (DEPTH, D_CONV), 0.01),
        "conv_ln_g": 1.0 + nrm(ks[15], (DEPTH, D_CONV), 0.02),
        "conv_ln_b": nrm(ks[16], (DEPTH, D_CONV), 0.01),
        "w_conv_out": nrm(ks[17], (DEPTH, D_CONV, D), D_CONV ** -0.5),
        "rel_bias": nrm(ks[18], (DEPTH, N_HEADS, 2 * WIN_ROWS - 1, 2 * WIN_COLS - 1), 0.1),
        "w_na_out": nrm(ks[19], (DEPTH, D_NA, D), D_NA ** -0.5),
        "w_out": nrm(ks[20], (DEPTH, D, D), D ** -0.5),
        "final_g": 1.0 + nrm(ks[21], (D,), 0.02),
    }


def reference(x_prompt, x_sample, cache_k, cache_v, c, c_ctx, w_ada, b_ada, norm_g,
              ffn_w_gate, ffn_w_up, ffn_w_down, w_in, conv_w, conv_b, conv_ln_g,
              conv_ln_b, w_conv_out, rel_bias, w_na_out, w_out, final_g):
    y_p = x_prompt
    y_s = x_sample
    silu_ctx = jax.nn.silu(c_ctx)
    silu_c = jax.nn.silu(c)
    keys_out = []
    vals_out = []
    for l in range(DEPTH):
        mod_p = jnp.split((silu_ctx @ w_ada[l] + b_ada[l])[None, None, :], N_MOD, axis=-1)
        mod_s = jnp.split((silu_c @ w_ada[l] + b_ada[l])[:, None, :], N_MOD, axis=-1)

        y_p = ffn_sublayer(y_p, mod_p[0], mod_p[1], mod_p[2], norm_g[l, 0],
                           ffn_w_gate[l, 0], ffn_w_up[l, 0], ffn_w_down[l, 0])
        y_s = ffn_sublayer(y_s, mod_s[0], mod_s[1], mod_s[2], norm_g[l, 0],
                           ffn_w_gate[l, 0], ffn_w_up[l, 0], ffn_w_down[l, 0])

        u_p = modulate(rmsnorm(y_p, norm_g[l, 1]), mod_p[3], mod_p[4])
        conv_in_p, q_p, k_p, v_p, gc_p, gn_p = in_projection(u_p, w_in[l])
        keys_out.append(k_p)
        vals_out.append(v_p)
        conv_h_p = conv_branch(conv_in_p, conv_w[l], conv_b[l], conv_ln_g[l], conv_ln_b[l])
        na_p = context_attention(q_p, k_p, v_p)
        y_p = y_p + mod_p[5] * merge_branches(conv_h_p, na_p, gc_p, gn_p,
                                              w_conv_out[l], w_na_out[l], w_out[l])

        u_s = modulate(rmsnorm(y_s, norm_g[l, 1]), mod_s[3], mod_s[4])
        conv_in_s, q_s, k_s, v_s, gc_s, gn_s = in_projection(u_s, w_in[l])
        conv_h_s = conv_branch(conv_in_s, conv_w[l], conv_b[l], conv_ln_g[l], conv_ln_b[l])
        na_s = neighbourhood_attention(q_s, k_s, v_s, cache_k[:, l], cache_v[:, l], rel_bias[l])
        y_s = y_s + mod_s[5] * merge_branches(conv_h_s, na_s, gc_s, gn_s,
                                              w_conv_out[l], w_na_out[l], w_out[l])

        y_p = ffn_sublayer(y_p, mod_p[6], mod_p[7], mod_p[8], norm_g[l, 2],
                           ffn_w_gate[l, 1], ffn_w_up[l, 1], ffn_w_down[l, 1])
        y_s = ffn_sublayer(y_s, mod_s[6], mod_s[7], mod_s[8], norm_g[l, 2],
                           ffn_w_gate[l, 1], ffn_w_up[l, 1], ffn_w_down[l, 1])

    y_prompt = rmsnorm(y_p, final_g)
    y_sample = rmsnorm(y_s, final_g)
    new_k = jnp.stack(keys_out, axis=1)
    new_v = jnp.stack(vals_out, axis=1)
    return (y_prompt, y_sample, new_k, new_v)
```

```python
import contextlib
import numpy as np
import concourse.bass as bass
import concourse.mybir as mybir
from concourse.bass_utils import run_bass_kernel_spmd

F32 = mybir.dt.float32
BF16 = mybir.dt.bfloat16
AF = mybir.ActivationFunctionType
ALU = mybir.AluOpType

D = 2048
C = 16
DFF = 5504
NF = 43
DEPTH = 2
NH = 16
EPS = 1e-6
NEG = -30000.0
NTOK = 3072
RING_SLOTS = 4
RING_W = 8192

ENGS = ("pe", "act", "dve", "pool", "sp")
DMA_K = {"pool": 8, "sp": 16}


class Buf:
    __slots__ = ("name", "w", "r", "rd", "excl")

    def __init__(self, name, excl=False):
        self.name = name
        self.excl = excl
        self.w = None
        self.r = {}
        self.rd = []


class Ins:
    __slots__ = ("eng", "fn", "deps", "flag", "val", "dma", "qi")


class Prog:
    def __init__(self):
        self.q = {e: [] for e in ENGS}
        self.ndma = {e: 0 for e in ENGS}

    def add(self, eng, fn, reads=(), writes=(), dma=False):
        ins = Ins()
        ins.eng = eng
        ins.fn = fn
        ins.dma = dma
        ins.flag = dma
        ins.val = None
        ins.qi = None
        if dma:
            ins.qi = self.ndma[eng]
            self.ndma[eng] += 1
        deps = []
        for b in reads:
            if b.w is not None:
                deps.append(b.w)
            if b.excl:
                deps.extend(r for e2, r in b.r.items() if e2 != eng)
        for b in writes:
            if b.w is not None:
                deps.append(b.w)
            deps.extend(b.r.values())
            deps.extend(b.rd)
        dd = []
        seen = set()
        for d in deps:
            if d is ins or id(d) in seen:
                continue
            seen.add(id(d))
            if (not d.dma) and (not dma) and d.eng == "pe" and eng == "pe":
                continue
            d.flag = True
            dd.append(d)
        ins.deps = dd
        for b in reads:
            if dma:
                b.rd.append(ins)
            else:
                b.r[eng] = ins
        for b in writes:
            b.w = ins
            b.r = {}
            b.rd = []
        self.q[eng].append(ins)
        return ins

    def handoff(self, olds, news):
        pend = []
        for b in olds:
            if b.w is not None:
                pend.append(b.w)
            pend.extend(b.r.values())
            pend.extend(b.rd)
            b.w = None
            b.r = {}
            b.rd = []
        for b in news:
            b.w = None
            b.r = {}
            b.rd = list(pend)


def emit_program(nc, P, esem, dsem, block):
    for e in ENGS:
        cnt = 0
        for ins in P.q[e]:
            if ins.dma:
                continue
            if ins.flag:
                cnt += 1
                ins.val = cnt

    def event(d):
        if d.dma:
            k = DMA_K[d.eng]
            return dsem[d.eng][d.qi % k], 16 * (d.qi // k + 1)
        return esem[d.eng], d.val

    def run(eng_name, e):
        waited = {}

        def wait(sem, val):
            key = id(sem)
            if waited.get(key, 0) >= val:
                return
            waited[key] = val
            e.wait_ge(sem, val)

        for ins in P.q[eng_name]:
            for d in ins.deps:
                s, v = event(d)
                wait(s, v)
            if ins.dma:
                k = DMA_K[eng_name]
                if ins.qi >= k:
                    wait(dsem[eng_name][ins.qi % k], 16 * (ins.qi // k))
                bi = ins.fn(e)
                bi.then_inc(dsem[eng_name][ins.qi % k], 16)
            else:
                bi = ins.fn(e)
                if ins.flag:
                    bi.then_inc(esem[eng_name], 1)
        if eng_name in DMA_K:
            k = DMA_K[eng_name]
            n = P.ndma[eng_name]
            for j in range(min(k, n)):
                cntj = len(range(j, n, k))
                wait(dsem[eng_name][j], 16 * cntj)

    @block.tensor
    def _(e):
        run("pe", e)

    @block.scalar
    def _(e):
        run("act", e)

    @block.vector
    def _(e):
        run("dve", e)

    @block.gpsimd
    def _(e):
        run("pool", e)

    @block.sync
    def _(e):
        run("sp", e)


def build_program(dbg=False, stages=None):
    nc = bass.Bass("TRN2", target_bir_lowering=False)
    P = Prog()
    nc._k_declared = None

    declared = []

    class LazyIn:
        def __init__(self, name, shape, dt):
            self.name, self.shape, self.dt, self._ap = name, list(shape), dt, None

        def get(self):
            if self._ap is None:
                self._ap = nc.dram_tensor(self.name, self.shape, self.dt, kind="ExternalInput").ap()
                declared.append(self.name)
            return self._ap

        def __getitem__(self, k):
            return self.get()[k]

        def rearrange(self, *a, **k):
            return self.get().rearrange(*a, **k)

    def din(name, shape, dt=F32):
        return LazyIn(name, shape, dt)

    def dout(name, shape, dt=F32):
        return nc.dram_tensor(name, list(shape), dt, kind="ExternalOutput").ap()

    def dscr(name, shape, dt):
        return nc.dram_tensor(name, list(shape), dt,
                              kind=("ExternalOutput" if dbg else "Internal")).ap()

    xp = din("xp", [512, D])
    xs = din("xs", [2560, D])
    ck_d = din("ck", [DEPTH, 512, 1024])
    cv_d = din("cv", [DEPTH, 512, 1024])
    cvec_d = din("cvec", [128, C, 2])
    w_ada = din("w_ada", [DEPTH, D, 9 * D])
    b_ada_d = din("b_ada_t", [128, DEPTH, 144])
    norm_g_d = din("norm_g_t", [128, DEPTH, 3, C])
    final_g_d = din("final_g_t", [128, C])
    wg = din("wg", [DEPTH, 2, D, DFF])
    wu = din("wu", [DEPTH, 2, D, DFF])
    wd = din("wd", [DEPTH, 2, DFF, D])
    w_in = din("w_in", [DEPTH, D, 9216])
    convw_d = din("convw_t", [128, 2, DEPTH, 8, 31])
    convv_d = din("convv_t", [128, 3, DEPTH, 8])
    w_co = din("w_co", [DEPTH, 1024, D])
    w_no = din("w_no", [DEPTH, 1024, D])
    w_o = din("w_o", [DEPTH, D, D])
    bias_first = din("bias_first", [DEPTH, NH, 4, 128, 640])
    bias_gen = din("bias_gen", [DEPTH, NH, 128, 640])
    ident_d = din("ident", [128, 128])

    yp = dout("yp", [512, D])
    ys = dout("ys", [2048, D])
    nk = dout("nk", [2, DEPTH, 256, 1024])
    nv = dout("nv", [2, DEPTH, 256, 1024])

    x1T_s = [dscr(f"x1T{l}", [C, 128, NTOK], F32) for l in range(DEPTH)]
    uT_s = [dscr(f"uT{l}", [C, 128, NTOK], BF16) for l in range(DEPTH)]
    hcT_s = [dscr(f"hcT{l}", [8, 128, NTOK], BF16) for l in range(DEPTH)]
    qT_s = [dscr(f"qT{l}", [8, 128, NTOK], BF16) for l in range(DEPTH)]
    kT_s = [dscr(f"kT{l}", [8, 128, NTOK], BF16) for l in range(DEPTH)]
    vt_s = [dscr(f"vt{l}", [NTOK, 1024], BF16) for l in range(DEPTH)]
    ckT_s = [dscr(f"ckT{l}", [8, 128, 512], BF16) for l in range(DEPTH)]
    cvb_s = [dscr(f"cvb{l}", [8, 128, 4, 128], BF16) for l in range(DEPTH)]

    def tilebufs(name):
        return [[Buf(f"{name}{l}_{t}") for t in range(NTOK // 512)] for l in range(DEPTH)]

    B_x1T = tilebufs("x1T")
    B_uT = tilebufs("uT")
    B_hcT = tilebufs("hcT")
    B_qT = tilebufs("qT")
    B_kT = tilebufs("kT")
    B_vt = tilebufs("vt")
    B_ckT = [Buf(f"ckT{l}") for l in range(DEPTH)]
    B_cvb = [Buf(f"cvb{l}") for l in range(DEPTH)]

    es = contextlib.ExitStack()
    with es:
        def sb(name, shape, dt):
            return es.enter_context(nc.sbuf_tensor("sb_" + name, list(shape), dt))

        ring = sb("ring", [128, RING_SLOTS, RING_W], BF16)
        xT = sb("xT", [128, C, 512], F32)
        hT = sb("hT", [128, C, 512], BF16)
        big = sb("big", [128, 32256], BF16)
        ident = sb("ident", [128, 128], F32)
        ones_bf = sb("ones_bf", [128, 128], BF16)
        scv = sb("scv", [128, C, 2], BF16)
        cvec = sb("cvec", [128, C, 2], F32)
        b_ada = sb("b_ada", [128, DEPTH, 144], F32)
        norm_g = sb("norm_g", [128, DEPTH, 3, C], F32)
        final_g = sb("final_g", [128, C], F32)
        convw = sb("convw", [128, 2, DEPTH, 8, 31], F32)
        convv = sb("convv", [128, 3, DEPTH, 8], F32)
        modv = sb("modv", [128, DEPTH, 2, 144], F32)
        dv = sb("dv", [128, DEPTH, 2, 9, C], F32)
        sq = sb("sq", [128, 2, 512], BF16)
        stdt = sb("stdt", [128, 512], F32)
        rstd = sb("rstd", [128, 512], F32)
        meant = sb("meant", [128, 512], F32)
        tmpf = sb("tmpf", [128, 2, 512], F32)
        ps = es.enter_context(nc.psum_tensor("ps", [128, 8, 512], F32))
        ps_flat = ps[:].rearrange("p a b -> p (a b)")

        esem = {e: es.enter_context(nc.semaphore(f"e_{e}")) for e in ("pe", "act", "dve")}
        dsem = {q: [es.enter_context(nc.semaphore(f"d_{q}{i}")) for i in range(k)]
                for q, k in DMA_K.items()}
        block = es.enter_context(nc.Block())

        PS = [Buf(f"ps{i}", excl=True) for i in range(8)]
        B_ring = [Buf(f"ring{i}") for i in range(RING_SLOTS)]
        B_xT = [Buf(f"xT{c}") for c in range(C)]
        B_hT = Buf("hT")
        B_sq = [Buf("sq0"), Buf("sq1")]
        B_std = Buf("std")
        B_rstd = Buf("rstd")
        B_mean = Buf("mean")
        B_tmpf = [Buf("tmpf0"), Buf("tmpf1")]
        B_const = Buf("const")
        B_dv = Buf("dv")
        B_modv = Buf("modv")
        B_scv = Buf("scv")

        def PE(fn, r=(), w=()):
            return P.add("pe", fn, r, w)

        def ACT(fn, r=(), w=()):
            return P.add("act", fn, r, w)

        def DVE(fn, r=(), w=()):
            return P.add("dve", fn, r, w)

        def SPD(out, in_, r=(), w=()):
            if isinstance(in_, LazyIn):
                in_ = in_.get()
            return P.add("sp", lambda e, o=out, i=in_: e.dma_start(out=o, in_=i), r, w, dma=True)

        def mm(out, lhsT, rhs, start, stop, r, w):
            return PE(lambda e, o=out, a=lhsT, b=rhs, s=start, t=stop:
                      e.matmul(o, a, b, start=s, stop=t), r, w)

        ring_n = [0]

        def ring_load(src, k, n):
            i = ring_n[0] % RING_SLOTS
            ring_n[0] += 1
            view = ring[:, i, 0:k * n].rearrange("p (k n) -> p k n", n=n)
            P.add("pool", lambda e, o=view, s=src: e.dma_start(out=o, in_=s),
                  (), (B_ring[i],), dma=True)
            return view, B_ring[i]

        def bview(a, b):
            return big[:, a:b]

        aT = bview(0, 22016).rearrange("p (f t) -> p f t", t=512)
        sgt = bview(22016, 24064).bitcast(F32).rearrange("p (a t) -> p a t", t=512)
        xstage = bview(24064, 32256).bitcast(F32).rearrange("p (a t) -> p a t", t=2048)
        B_aT = [Buf(f"aT{f}") for f in range(NF)]
        B_sgt = [Buf("sgt0"), Buf("sgt1")]
        B_xstage = [Buf("xst0"), Buf("xst1")]
        FFN_BUFS = B_aT + B_sgt + B_xstage
        st_bf = bview(0, 4096).rearrange("p (a t) -> p a t", t=512)
        st_f32 = bview(4096, 8192).bitcast(F32).rearrange("p (a t) -> p a t", t=512)
        B_stbf = [Buf(f"stbf{i}") for i in range(8)]
        B_stf = [Buf(f"stf{i}") for i in range(4)]
        IP_BUFS = B_stbf + B_stf
        convh = bview(0, 4096).rearrange("p (a t) -> p a t", t=512)
        naoT = bview(4096, 8192).rearrange("p (a t) -> p a t", t=512)
        Y0 = 8192
        convacc = bview(Y0, Y0 + 8192).bitcast(F32).rearrange("p (a t) -> p a t", t=512)
        hcbuf = bview(Y0 + 8192, Y0 + 8192 + 4608).rearrange("p (a t) -> p a t", t=576)
        cbf = bview(Y0 + 12800, Y0 + 12800 + 2048).rearrange("p (a t) -> p a t", t=512)
        B_convh = [Buf(f"convh{i}") for i in range(8)]
        B_naoT = [Buf(f"naoT{i}") for i in range(8)]
        B_convacc = [Buf(f"convacc{i}") for i in range(8)]
        B_hcbuf = Buf("hcbuf")
        B_cbf = [Buf(f"cbf{i}") for i in range(4)]
        CONV_BUFS = B_convacc + [B_hcbuf] + B_cbf
        o = Y0
        kb = bview(o, o + 2048).rearrange("p (a t) -> p a t", t=1024); o += 2048
        vb = bview(o, o + 2048).rearrange("p (a c d) -> p a c d", c=8, d=128); o += 2048
        qb = bview(o, o + 1024).rearrange("p (a t) -> p a t", t=512); o += 1024
        ckb = bview(o, o + 1024).rearrange("p (a t) -> p a t", t=512); o += 1024
        cvbuf = bview(o, o + 1024).rearrange("p (a c d) -> p a c d", c=4, d=128); o += 1024
        biasb = bview(o, o + 2560).bitcast(F32).rearrange("p (a t) -> p a t", t=640); o += 2560
        sbb = bview(o, o + 2560).bitcast(F32).rearrange("p (a t) -> p a t", t=640); o += 2560
        pb = bview(o, o + 1280).rearrange("p (a t) -> p a t", t=640); o += 1280
        pc = bview(o, o + 1024).rearrange("p (a t) -> p a t", t=512); o += 1024
        rcp = bview(o, o + 1024).bitcast(F32); o += 1024
        assert o <= 32256
        B_kvq = [Buf("kvq0"), Buf("kvq1")]
        B_biasb = [Buf("biasb0"), Buf("biasb1")]
        B_sbb = [Buf("sbb0"), Buf("sbb1")]
        B_pb = [Buf("pb0"), Buf("pb1")]
        B_pc = [Buf("pc0"), Buf("pc1")]
        B_rcp = Buf("rcp")
        ATT_BUFS = B_kvq + B_biasb + B_sbb + B_pb + B_pc + [B_rcp]
        t1 = bview(Y0, Y0 + 4096).bitcast(F32).rearrange("p (a t) -> p a t", t=512)
        mT = bview(Y0 + 4096, Y0 + 4096 + 8192).rearrange("p (a t) -> p a t", t=512)
        B_t1 = [Buf(f"t1_{i}") for i in range(4)]
        B_mT = [Buf(f"mT{i}") for i in range(C)]
        MRG_BUFS = B_t1 + B_mT
        ostage = bview(0, 8192).bitcast(F32).rearrange("p (a t) -> p a t", t=2048)
        B_ostage = [Buf("ost0"), Buf("ost1")]
        cstage = bview(0, 8192).bitcast(F32).rearrange("p (c t) -> p c t", t=1024)
        cstb = bview(8192, 12288).rearrange("p (c t) -> p c t", t=1024)
        ckTt = bview(12288, 16384).rearrange("p (c t) -> p c t", t=512)
        B_cstage = Buf("cstage")
        B_cstb = Buf("cstb")
        B_ckTt = Buf("ckTt")

        cur_big = [[]]

        def switch_big(news):
            P.handoff(cur_big[0], news)
            cur_big[0] = list(news)

        SPD(ident[:], ident_d, (), (B_const,))
        SPD(cvec[:], cvec_d, (), (B_const,))
        SPD(b_ada[:], b_ada_d, (), (B_const,))
        SPD(norm_g[:], norm_g_d, (), (B_const,))
        SPD(final_g[:], final_g_d, (), (B_const,))
        SPD(convw[:], convw_d, (), (B_const,))
        SPD(convv[:], convv_d, (), (B_const,))
        DVE(lambda e: e.memset(ones_bf[:], 1.0), (), (B_const,))
        ACT(lambda e: e.activation(scv[:], cvec[:], AF.Silu), (B_const,), (B_scv,))

        st = stages or {}
        psm = ps[:, 0, 0:288]
        for l in (range(DEPTH) if st.get("ada", True) else []):
            wv = w_ada[l].rearrange("(c p) n -> p c n", p=128)
            for blk in range(36):
                view, rb = ring_load(wv[:, :, blk * 512:(blk + 1) * 512], C, 512)
                for j in range(4):
                    col = (blk * 4 + j) * 2
                    for kc in range(C):
                        mm(psm[:, col:col + 2], view[:, kc, j * 128:(j + 1) * 128], scv[:, kc, :],
                           kc == 0, kc == C - 1, (rb, B_scv), (PS[0],))
            psm3 = psm.rearrange("p (j s) -> p j s", s=2)
            for s in range(2):
                DVE(lambda e, l=l, s=s, psm3=psm3: e.tensor_tensor(
                    modv[:, l, s, :], psm3[:, :, s], b_ada[:, l, :], ALU.add),
                    (PS[0], B_const), (B_modv,))
            for s in range(2):
                for k3 in range(3):
                    shift = modv[:, l, s, (3 * k3) * C:(3 * k3 + 1) * C]
                    scale = modv[:, l, s, (3 * k3 + 1) * C:(3 * k3 + 2) * C]
                    gate = modv[:, l, s, (3 * k3 + 2) * C:(3 * k3 + 3) * C]
                    DVE(lambda e, l=l, s=s, k3=k3, scale=scale: e.scalar_tensor_tensor(
                        dv[:, l, s, 3 * k3, :], scale, 1.0, norm_g[:, l, k3, :], ALU.add, ALU.mult),
                        (B_modv, B_const), (B_dv,))
                    DVE(lambda e, l=l, s=s, k3=k3, shift=shift: e.tensor_copy(
                        dv[:, l, s, 3 * k3 + 1, :], shift), (B_modv,), (B_dv,))
                    gsc = 1.0 if k3 == 1 else 0.5
                    DVE(lambda e, l=l, s=s, k3=k3, gate=gate, gsc=gsc: e.tensor_scalar(
                        dv[:, l, s, 3 * k3 + 2, :], gate, gsc, None, ALU.mult),
                        (B_modv,), (B_dv,))

        switch_big([B_cstage, B_cstb, B_ckTt])
        for l in (range(DEPTH) if st.get("cache", True) else []):
            SPD(cstage, ck_d[l].rearrange("(c p) n -> p c n", p=128), (), (B_cstage,))
            n = 0
            for fc in range(8):
                for tc in range(4):
                    bank = 4 + (n // 4) % 4
                    sl = n % 4
                    PE(lambda e, bank=bank, sl=sl, tc=tc, fc=fc: e.transpose(
                        ps[:, bank, sl * 128:(sl + 1) * 128], cstage[:, tc, fc * 128:(fc + 1) * 128],
                        ident[:]), (B_cstage, B_const), (PS[bank],))
                    n += 1
                    if sl == 3:
                        ACT(lambda e, bank=bank, fc=fc: e.activation(
                            ckTt[:, fc, :], ps[:, bank, :], AF.Copy), (PS[bank],), (B_ckTt,))
            SPD(ckT_s[l].rearrange("c p t -> p c t"), ckTt, (B_ckTt,), (B_ckT[l],))
            SPD(cstage, cv_d[l].rearrange("(c p) n -> p c n", p=128), (), (B_cstage,))
            DVE(lambda e: e.tensor_copy(cstb, cstage), (B_cstage,), (B_cstb,))
            SPD(cvb_s[l].rearrange("j p c d -> p c j d"),
                cstb.rearrange("p c (j d) -> p c j d", d=128), (B_cstb,), (B_cvb[l],))

        def rms_stats(nt):
            for c in range(C):
                DVE(lambda e, c=c: e.tensor_tensor(sq[:, c % 2, :nt], xT[:, c, :nt], xT[:, c, :nt],
                                                   ALU.mult), (B_xT[c],), (B_sq[c % 2],))
                mm(ps[:, 7, :nt], ones_bf[:], sq[:, c % 2, :nt], c == 0, c == C - 1,
                   (B_sq[c % 2], B_const), (PS[7],))
            ACT(lambda e: e.activation(stdt[:, :nt], ps[:, 7, :nt], AF.Sqrt, bias=EPS, scale=1.0 / D),
                (PS[7],), (B_std,))
            DVE(lambda e: e.reciprocal(rstd[:, :nt], stdt[:, :nt]), (B_std,), (B_rstd,))

        def modulate(nt, l, s, k3):
            A = dv[:, l, s, 3 * k3, :]
            Bv = dv[:, l, s, 3 * k3 + 1, :]
            for c in range(C):
                DVE(lambda e, c=c, A=A: e.scalar_tensor_tensor(
                    tmpf[:, c % 2, :nt], xT[:, c, :nt], A[:, c:c + 1], rstd[:, :nt], ALU.mult, ALU.mult),
                    (B_xT[c], B_rstd, B_dv), (B_tmpf[c % 2],))
                ACT(lambda e, c=c, Bv=Bv: e.activation(
                    hT[:, c, :nt], tmpf[:, c % 2, :nt], AF.Identity, bias=Bv[:, c:c + 1]),
                    (B_tmpf[c % 2], B_dv), (B_hT,))

        def ffn(nt, l, w, s):
            k3 = 0 if w == 0 else 2
            switch_big(FFN_BUFS)
            rms_stats(nt)
            modulate(nt, l, s, k3)
            wgv = wg[l, w].rearrange("(c p) n -> p c n", p=128)
            wuv = wu[l, w].rearrange("(c p) n -> p c n", p=128)
            wdv = wd[l, w].rearrange("(f p) n -> p f n", p=128)
            G = dv[:, l, s, 3 * k3 + 2, :]
            for blk in range(11):
                c0 = blk * 512
                wb = min(512, DFF - c0)
                gv, gb = ring_load(wgv[:, :, c0:c0 + wb], C, wb)
                uv, ub = ring_load(wuv[:, :, c0:c0 + wb], C, wb)
                for j in range(wb // 128):
                    f = blk * 4 + j
                    bg = (f % 2) * 2
                    bu = bg + 1
                    for kc in range(C):
                        mm(ps[:, bg, :nt], gv[:, kc, j * 128:(j + 1) * 128], hT[:, kc, :nt],
                           kc == 0, kc == C - 1, (gb, B_hT), (PS[bg],))
                    for kc in range(C):
                        mm(ps[:, bu, :nt], uv[:, kc, j * 128:(j + 1) * 128], hT[:, kc, :nt],
                           kc == 0, kc == C - 1, (ub, B_hT), (PS[bu],))
                    ACT(lambda e, f=f, bg=bg: e.activation(sgt[:, f % 2, :nt], ps[:, bg, :nt], AF.Silu),
                        (PS[bg],), (B_sgt[f % 2],))
                    DVE(lambda e, f=f, bu=bu: e.tensor_tensor(
                        aT[:, f, :nt], sgt[:, f % 2, :nt], ps[:, bu, :nt], ALU.mult),
                        (B_sgt[f % 2], PS[bu]), (B_aT[f],))
            for dp in range(8):
                banks = (4, 5) if dp % 2 == 0 else (6, 7)
                for half in range(2):
                    f0, f1 = (0, 22) if half == 0 else (22, NF)
                    dvw, db = ring_load(wdv[:, f0:f1, dp * 256:(dp + 1) * 256], f1 - f0, 256)
                    for dj in range(2):
                        for f in range(f0, f1):
                            mm(ps[:, banks[dj], :nt], dvw[:, f - f0, dj * 128:(dj + 1) * 128],
                               aT[:, f, :nt], f == 0, f == NF - 1, (db, B_aT[f]), (PS[banks[dj]],))
                for dj in range(2):
                    c = dp * 2 + dj
                    DVE(lambda e, c=c, bk=banks[dj], G=G: e.scalar_tensor_tensor(
                        xT[:, c, :nt], ps[:, bk, :nt], G[:, c:c + 1], xT[:, c, :nt], ALU.mult, ALU.add),
                        (PS[banks[dj]], B_xT[c], B_dv), (B_xT[c],))

        def load_x_input(tile):
            switch_big(FFN_BUFS)
            src = xp if tile == 0 else xs
            r0 = 0 if tile == 0 else (tile - 1) * 512
            n = 0
            for sub in range(4):
                SPD(xstage[:, sub % 2, :], src[r0 + sub * 128: r0 + (sub + 1) * 128, :],
                    (), (B_xstage[sub % 2],))
                for c4 in range(4):
                    bank = 4 + n % 4
                    n += 1
                    for q in range(4):
                        c = c4 * 4 + q
                        PE(lambda e, bank=bank, q=q, c=c, sub=sub: e.transpose(
                            ps[:, bank, q * 128:(q + 1) * 128], xstage[:, sub % 2, c * 128:(c + 1) * 128],
                            ident[:]), (B_xstage[sub % 2], B_const), (PS[bank],))
                    eng = ACT if c4 % 2 == 0 else DVE
                    dst = xT[:, c4 * 4:(c4 + 1) * 4, sub * 128:(sub + 1) * 128]
                    srcp = ps[:, bank, :].rearrange("p (q t) -> p q t", t=128)
                    if c4 % 2 == 0:
                        ACT(lambda e, dst=dst, srcp=srcp: e.activation(dst, srcp, AF.Copy),
                            (PS[bank],), tuple(B_xT[c4 * 4:(c4 + 1) * 4]))
                    else:
                        DVE(lambda e, dst=dst, srcp=srcp: e.tensor_copy(dst, srcp),
                            (PS[bank],), tuple(B_xT[c4 * 4:(c4 + 1) * 4]))

        def store_x1(l, tile, nt):
            g0 = tile * 512
            SPD(x1T_s[l].rearrange("c p t -> p c t")[:, :, g0:g0 + nt], xT[:, :, :nt],
                tuple(B_xT), (B_x1T[l][tile],))

        def load_x1(l, tile, nt):
            g0 = tile * 512
            SPD(xT[:, :, :nt], x1T_s[l].rearrange("c p t -> p c t")[:, :, g0:g0 + nt],
                (B_x1T[l][tile],), tuple(B_xT))
            SPD(hT[:, :, :nt], uT_s[l].rearrange("c p t -> p c t")[:, :, g0:g0 + nt],
                (B_uT[l][tile],), (B_hT,))

        def in_proj1(nt, l, s, tile):
            is_p = tile == 0 and not st.get("noprompt", False)
            g0 = tile * 512
            rms_stats(nt)
            modulate(nt, l, s, 1)
            switch_big(IP_BUFS)
            SPD(uT_s[l].rearrange("c p t -> p c t")[:, :, g0:g0 + nt], hT[:, :, :nt],
                (B_hT,), (B_uT[l][tile],))
            wv = w_in[l].rearrange("(c p) n -> p c n", p=128)
            nst = [0]
            nsf = [0]
            npb = [0]

            def stb():
                i = nst[0] % 8
                nst[0] += 1
                return st_bf[:, i, :], B_stbf[i]

            def stf():
                i = nsf[0] % 4
                nsf[0] += 1
                return st_f32[:, i, :], B_stf[i]

            def pbank(lo, n):
                i = lo + npb[0] % n
                npb[0] += 1
                return i

            for half in range(2):
                av, ab = ring_load(wv[:, :, half * 512:(half + 1) * 512], C, 512)
                gvw, gbb = ring_load(wv[:, :, 1024 + half * 512:1024 + (half + 1) * 512], C, 512)
                for j in range(4):
                    ch = half * 4 + j
                    ba = (ch % 2) * 2
                    bgk = ba + 1
                    for kc in range(C):
                        mm(ps[:, ba, :nt], av[:, kc, j * 128:(j + 1) * 128], hT[:, kc, :nt],
                           kc == 0, kc == C - 1, (ab, B_hT), (PS[ba],))
                    for kc in range(C):
                        mm(ps[:, bgk, :nt], gvw[:, kc, j * 128:(j + 1) * 128], hT[:, kc, :nt],
                           kc == 0, kc == C - 1, (gbb, B_hT), (PS[bgk],))
                    sv, sbuf_ = stf()
                    ACT(lambda e, sv=sv, bgk=bgk: e.activation(sv[:, :nt], ps[:, bgk, :nt], AF.Sigmoid),
                        (PS[bgk],), (sbuf_,))
                    ov, obuf = stb()
                    DVE(lambda e, ov=ov, sv=sv, ba=ba: e.tensor_tensor(
                        ov[:, :nt], sv[:, :nt], ps[:, ba, :nt], ALU.mult), (sbuf_, PS[ba]), (obuf,))
                    SPD(hcT_s[l][ch, :, g0:g0 + nt], ov[:, :nt], (obuf,), (B_hcT[l][tile],))
            for which in range(2):
                dst_s = qT_s[l] if which == 0 else kT_s[l]
                dst_b = B_qT[l][tile] if which == 0 else B_kT[l][tile]
                for half in range(2):
                    cb0 = 2048 + which * 1024 + half * 512
                    bv, bb = ring_load(wv[:, :, cb0:cb0 + 512], C, 512)
                    for j in range(4):
                        ch = half * 4 + j
                        bk = pbank(0, 4)
                        for kc in range(C):
                            mm(ps[:, bk, :nt], bv[:, kc, j * 128:(j + 1) * 128], hT[:, kc, :nt],
                               kc == 0, kc == C - 1, (bb, B_hT), (PS[bk],))
                        ov, obuf = stb()
                        sc = 0.125 if which == 0 else 1.0
                        ACT(lambda e, ov=ov, bk=bk, sc=sc: e.activation(
                            ov[:, :nt], ps[:, bk, :nt], AF.Copy, scale=sc), (PS[bk],), (obuf,))
                        SPD(dst_s[ch, :, g0:g0 + nt], ov[:, :nt], (obuf,), (dst_b,))
                    if which == 1 and is_p:
                        for sub in range(nt // 128):
                            bk = pbank(4, 4)
                            for kc in range(C):
                                mm(ps[:, bk, :], hT[:, kc, sub * 128:(sub + 1) * 128], bv[:, kc, :],
                                   kc == 0, kc == C - 1, (bb, B_hT), (PS[bk],))
                            fv, fbuf = stf()
                            DVE(lambda e, fv=fv, bk=bk: e.tensor_copy(fv, ps[:, bk, :]), (PS[bk],), (fbuf,))
                            b_i, t0 = sub // 2, (sub % 2) * 128
                            if st.get("nk", True):
                                SPD(nk[b_i, l, t0:t0 + 128, half * 512:(half + 1) * 512], fv, (fbuf,), ())
            for half in range(2):
                cb0 = 4096 + half * 512
                bv, bb = ring_load(wv[:, :, cb0:cb0 + 512], C, 512)
                for sub in range(nt // 128):
                    bk = pbank(4, 4)
                    for kc in range(C):
                        mm(ps[:, bk, :], hT[:, kc, sub * 128:(sub + 1) * 128], bv[:, kc, :],
                           kc == 0, kc == C - 1, (bb, B_hT), (PS[bk],))
                    ov, obuf = stb()
                    ACT(lambda e, ov=ov, bk=bk: e.activation(ov, ps[:, bk, :], AF.Copy), (PS[bk],), (obuf,))
                    SPD(vt_s[l][g0 + sub * 128:g0 + (sub + 1) * 128, half * 512:(half + 1) * 512], ov,
                        (obuf,), (B_vt[l][tile],))
                    if is_p:
                        fv, fbuf = stf()
                        DVE(lambda e, fv=fv, bk=bk: e.tensor_copy(fv, ps[:, bk, :]), (PS[bk],), (fbuf,))
                        b_i, t0 = sub // 2, (sub % 2) * 128
                        if st.get("nv", True):
                            SPD(nv[b_i, l, t0:t0 + 128, half * 512:(half + 1) * 512], fv, (fbuf,), ())

        def conv_branch(nt, l, tile):
            is_p = tile == 0
            g0 = tile * 512
            switch_big(B_convh + B_naoT + CONV_BUFS)
            hsrc = hcT_s[l].rearrange("c p t -> p c t")
            if is_p:
                hb3 = hcbuf[:, :, 0:572].rearrange("p a (b t) -> p a b t", t=286)
                DVE(lambda e: e.memset(hcbuf[:, :, 0:572], 0.0), (), (B_hcbuf,))
                for b_i in range(2):
                    SPD(hb3[:, :, b_i, 15:271], hsrc[:, :, g0 + b_i * 256:g0 + (b_i + 1) * 256],
                        (B_hcT[l][tile],), (B_hcbuf,))
            else:
                lt0 = (tile - 1) * 512
                if lt0 == 0:
                    DVE(lambda e: e.memset(hcbuf[:, :, 0:15], 0.0), (), (B_hcbuf,))
                    SPD(hcbuf[:, :, 15:15 + nt + 15], hsrc[:, :, g0:g0 + nt + 15],
                        (B_hcT[l][tile], B_hcT[l][tile + 1]), (B_hcbuf,))
                else:
                    rd = [B_hcT[l][tile - 1], B_hcT[l][tile]]
                    if tile + 1 < NTOK // 512:
                        rd.append(B_hcT[l][tile + 1])
                    SPD(hcbuf[:, :, 0:nt + 30], hsrc[:, :, g0 - 15:g0 + nt + 15], tuple(rd), (B_hcbuf,))
            wset = 0 if is_p else 1
            cbv = convv[:, 0, l, :]
            lng = convv[:, 1, l, :]
            lnb = convv[:, 2, l, :]
            for chp in range(4):
                for j in range(31):
                    for q in range(2):
                        ch = chp * 2 + q
                        wj = convw[:, wset, l, ch, j:j + 1]
                        if is_p:
                            acc = convacc[:, ch, :].rearrange("p (b t) -> p b t", t=256)
                            src = hb3[:, ch, :, j:j + 256]
                        else:
                            acc = convacc[:, ch, :nt]
                            src = hcbuf[:, ch, j:j + nt]
                        if j == 0:
                            DVE(lambda e, acc=acc, src=src, wj=wj, ch=ch: e.tensor_scalar(
                                acc, src, wj, cbv[:, ch:ch + 1], ALU.mult, ALU.add),
                                (B_hcbuf, B_const), (B_convacc[ch],))
                        else:
                            DVE(lambda e, acc=acc, src=src, wj=wj: e.scalar_tensor_tensor(
                                acc, src, wj, acc, ALU.mult, ALU.add),
                                (B_hcbuf, B_const, B_convacc[ch]), (B_convacc[ch],))
            for ch in range(8):
                i0 = (2 * ch) % 4
                i1 = (2 * ch + 1) % 4
                ACT(lambda e, ch=ch, i0=i0: e.activation(cbf[:, i0, :nt], convacc[:, ch, :nt], AF.Copy),
                    (B_convacc[ch],), (B_cbf[i0],))
                DVE(lambda e, ch=ch, i1=i1: e.tensor_tensor(
                    cbf[:, i1, :nt], convacc[:, ch, :nt], convacc[:, ch, :nt], ALU.mult),
                    (B_convacc[ch],), (B_cbf[i1],))
                mm(ps[:, 4, :nt], ones_bf[:], cbf[:, i0, :nt], ch == 0, ch == 7, (B_cbf[i0], B_const), (PS[4],))
                mm(ps[:, 5, :nt], ones_bf[:], cbf[:, i1, :nt], ch == 0, ch == 7, (B_cbf[i1], B_const), (PS[5],))
            ACT(lambda e: e.activation(meant[:, :nt], ps[:, 4, :nt], AF.Copy, scale=1.0 / 1024),
                (PS[4],), (B_mean,))
            DVE(lambda e: e.tensor_tensor(tmpf[:, 0, :nt], meant[:, :nt], meant[:, :nt], ALU.mult),
                (B_mean,), (B_tmpf[0],))
            DVE(lambda e: e.scalar_tensor_tensor(tmpf[:, 1, :nt], ps[:, 5, :nt], 1.0 / 1024, tmpf[:, 0, :nt],
                                                 ALU.mult, ALU.subtract), (PS[5], B_tmpf[0]), (B_tmpf[1],))
            ACT(lambda e: e.activation(stdt[:, :nt], tmpf[:, 1, :nt], AF.Sqrt, bias=EPS), (B_tmpf[1],), (B_std,))
            DVE(lambda e: e.reciprocal(rstd[:, :nt], stdt[:, :nt]), (B_std,), (B_rstd,))
            for ch in range(8):
                DVE(lambda e, ch=ch: e.tensor_tensor(convacc[:, ch, :nt], convacc[:, ch, :nt], meant[:, :nt],
                                                     ALU.subtract), (B_convacc[ch], B_mean), (B_convacc[ch],))
                DVE(lambda e, ch=ch: e.tensor_tensor(convacc[:, ch, :nt], convacc[:, ch, :nt], rstd[:, :nt],
                                                     ALU.mult), (B_convacc[ch], B_rstd), (B_convacc[ch],))
                ACT(lambda e, ch=ch: e.activation(convh[:, ch, :nt], convacc[:, ch, :nt], AF.Silu,
                                                  bias=lnb[:, ch:ch + 1], scale=lng[:, ch:ch + 1]),
                    (B_convacc[ch], B_const), (B_convh[ch],))

        def attention(nt, l, tile):
            is_p = tile == 0
            g0 = tile * 512
            switch_big(B_convh + B_naoT + ATT_BUFS)
            qsrc = qT_s[l]
            ksrc = kT_s[l]
            npair = nt // 128
            if is_p:
                kt0, nkc = g0, 4
                kbufs = [B_kT[l][0]]
                vbufs = [B_vt[l][0]]
            else:
                m0 = (tile - 1) * 4
                cm_lo = max(m0 - 2, 0)
                cm_hi = max(m0 + npair - 1 - 2, 0) + 4
                nkc = cm_hi - cm_lo + 1
                kt0 = 512 + cm_lo * 128
                tl = sorted(set((kt0 + i * 128) // 512 for i in range(nkc)))
                kbufs = [B_kT[l][t] for t in tl]
                vbufs = [B_vt[l][t] for t in tl]
            sctr = [0]
            xctr = [0]

            def loads(j):
                bj = B_kvq[j % 2]
                SPD(qb[:, j % 2, :nt], qsrc[j, :, g0:g0 + nt], (B_qT[l][tile],), (bj,))
                SPD(kb[:, j % 2, :nkc * 128], ksrc[j, :, kt0:kt0 + nkc * 128], tuple(kbufs), (bj,))
                SPD(vb[:, j % 2, :nkc, :],
                    vt_s[l][kt0:kt0 + nkc * 128, j * 128:(j + 1) * 128].rearrange("(c p) d -> p c d", p=128),
                    tuple(vbufs), (bj,))
                if not is_p:
                    SPD(ckb[:, j % 2, :], ckT_s[l][j], (B_ckT[l],), (bj,))
                    SPD(cvbuf[:, j % 2, :, :], cvb_s[l][j], (B_cvb[l],), (bj,))

            loads(0)
            for j in range(8):
                if j + 1 < 8:
                    loads(j + 1)
                bj = B_kvq[j % 2]
                jb = j % 2
                for hh in range(2):
                    h = 2 * j + hh
                    pr = slice(hh * 64, hh * 64 + 64)
                    bO = 4 + hh * 2
                    bD = bO + 1
                    if is_p:
                        for b_i in range(2):
                            qs = slice(b_i * 256, (b_i + 1) * 256)
                            for kk in range(2):
                                kc = b_i * 2 + kk
                                bk = sctr[0] % 4
                                sctr[0] += 1
                                mm(ps[:, bk, 0:256], kb[pr, jb, kc * 128:(kc + 1) * 128], qb[pr, jb, qs],
                                   True, True, (bj,), (PS[bk],))
                                x = xctr[0] % 2
                                xctr[0] += 1
                                ACT(lambda e, x=x, bk=bk: e.activation(pc[:, x, 0:256], ps[:, bk, 0:256], AF.Exp),
                                    (PS[bk],), (B_pc[x],))
                                mm(ps[:, bO, qs], vb[:, jb, kc, :], pc[:, x, 0:256], kk == 0, kk == 1,
                                   (bj, B_pc[x]), (PS[bO],))
                                mm(ps[:, bD, qs], ones_bf[:], pc[:, x, 0:256], kk == 0, kk == 1,
                                   (B_pc[x], B_const), (PS[bD],))
                    else:
                        for cc in range(4):
                            bk = sctr[0] % 4
                            sctr[0] += 1
                            mm(ps[:, bk, :nt], ckb[pr, jb, cc * 128:(cc + 1) * 128], qb[pr, jb, :nt],
                               True, True, (bj,), (PS[bk],))
                            x = xctr[0] % 2
                            xctr[0] += 1
                            ACT(lambda e, x=x, bk=bk: e.activation(pc[:, x, :nt], ps[:, bk, :nt], AF.Exp),
                                (PS[bk],), (B_pc[x],))
                            mm(ps[:, bO, :nt], cvbuf[:, jb, cc, :], pc[:, x, :nt], cc == 0, False,
                               (bj, B_pc[x]), (PS[bO],))
                            mm(ps[:, bD, :nt], ones_bf[:], pc[:, x, :nt], cc == 0, False,
                               (B_pc[x], B_const), (PS[bD],))
                        for pi in range(npair):
                            m = m0 + pi
                            cm0 = max(m - 2, 0)
                            x = xctr[0] % 2
                            xctr[0] += 1
                            bsrc = bias_first[l, h, m] if m < 4 else bias_gen[l, h]
                            SPD(biasb[:, x, :], bsrc, (), (B_biasb[x],))
                            pb0 = (sctr[0] % 2) * 2
                            sctr[0] += 1
                            sreg = ps_flat[:, pb0 * 512: pb0 * 512 + 640]
                            qs = slice(pi * 128, (pi + 1) * 128)
                            for ci in range(5):
                                kc = cm0 + ci - cm_lo
                                mm(sreg[:, ci * 128:(ci + 1) * 128], kb[pr, jb, kc * 128:(kc + 1) * 128],
                                   qb[pr, jb, qs], True, True, (bj,), (PS[pb0], PS[pb0 + 1]))
                            DVE(lambda e, x=x, sreg=sreg: e.tensor_tensor(
                                sbb[:, x, 0:512], sreg[:, 0:512], biasb[:, x, 0:512], ALU.add),
                                (PS[pb0], B_biasb[x]), (B_sbb[x],))
                            DVE(lambda e, x=x, sreg=sreg: e.tensor_tensor(
                                sbb[:, x, 512:640], sreg[:, 512:640], biasb[:, x, 512:640], ALU.add),
                                (PS[pb0 + 1], B_biasb[x]), (B_sbb[x],))
                            ACT(lambda e, x=x: e.activation(pb[:, x, :], sbb[:, x, :], AF.Exp),
                                (B_sbb[x],), (B_pb[x],))
                            for ci in range(5):
                                kc = cm0 + ci - cm_lo
                                last = ci == 4
                                mm(ps[:, bO, qs], vb[:, jb, kc, :], pb[:, x, ci * 128:(ci + 1) * 128], False, last,
                                   (bj, B_pb[x]), (PS[bO],))
                                mm(ps[:, bD, qs], ones_bf[:], pb[:, x, ci * 128:(ci + 1) * 128], False, last,
                                   (B_pb[x], B_const), (PS[bD],))
                    DVE(lambda e, pr=pr, bD=bD: e.reciprocal(rcp[pr, :nt], ps[pr, bD, :nt]), (PS[bD],), (B_rcp,))
                    DVE(lambda e, pr=pr, bO=bO, j=j: e.tensor_tensor(
                        naoT[pr, j, :nt], ps[pr, bO, :nt], rcp[pr, :nt], ALU.mult),
                        (PS[bO], B_rcp), (B_naoT[j],))

        def merge_out(nt, l, s):
            switch_big(B_convh + B_naoT + MRG_BUFS)
            wv = w_in[l].rearrange("(c p) n -> p c n", p=128)
            cov = w_co[l].rearrange("(c p) n -> p c n", p=128)
            nov = w_no[l].rearrange("(c p) n -> p c n", p=128)
            wov = w_o[l].rearrange("(c p) n -> p c n", p=128)
            G = dv[:, l, s, 5, :]
            n = [0]
            for g4 in range(4):
                for br in range(2):
                    gcol = (5120 if br == 0 else 7168) + g4 * 512
                    gv, gb = ring_load(wv[:, :, gcol:gcol + 512], C, 512)
                    pv, pbuf = ring_load((cov if br == 0 else nov)[:, :, g4 * 512:(g4 + 1) * 512], 8, 512)
                    act_in = convh if br == 0 else naoT
                    act_b = B_convh if br == 0 else B_naoT
                    for jj in range(4):
                        c = g4 * 4 + jj
                        ba = (n[0] % 2) * 2
                        bb_ = ba + 1
                        n[0] += 1
                        for kc in range(C):
                            mm(ps[:, ba, :nt], gv[:, kc, jj * 128:(jj + 1) * 128], hT[:, kc, :nt],
                               kc == 0, kc == C - 1, (gb, B_hT), (PS[ba],))
                        for kc in range(8):
                            mm(ps[:, bb_, :nt], pv[:, kc, jj * 128:(jj + 1) * 128], act_in[:, kc, :nt],
                               kc == 0, kc == 7, (pbuf, act_b[kc]), (PS[bb_],))
                        x = n[0] % 2
                        ACT(lambda e, x=x, ba=ba: e.activation(tmpf[:, x, :nt], ps[:, ba, :nt], AF.Sigmoid),
                            (PS[ba],), (B_tmpf[x],))
                        if br == 0:
                            DVE(lambda e, x=x, bb_=bb_, jj=jj: e.tensor_tensor(
                                t1[:, jj, :nt], tmpf[:, x, :nt], ps[:, bb_, :nt], ALU.mult),
                                (B_tmpf[x], PS[bb_]), (B_t1[jj],))
                        else:
                            DVE(lambda e, x=x, bb_=bb_: e.tensor_tensor(
                                tmpf[:, x, :nt], tmpf[:, x, :nt], ps[:, bb_, :nt], ALU.mult),
                                (B_tmpf[x], PS[bb_]), (B_tmpf[x],))
                            DVE(lambda e, x=x, jj=jj, c=c: e.tensor_tensor(
                                mT[:, c, :nt], tmpf[:, x, :nt], t1[:, jj, :nt], ALU.add),
                                (B_tmpf[x], B_t1[jj]), (B_mT[c],))
            for g4 in range(4):
                ov, ob = ring_load(wov[:, :, g4 * 512:(g4 + 1) * 512], C, 512)
                for jj in range(4):
                    c = g4 * 4 + jj
                    bk = 4 + c % 4
                    for kc in range(C):
                        mm(ps[:, bk, :nt], ov[:, kc, jj * 128:(jj + 1) * 128], mT[:, kc, :nt],
                           kc == 0, kc == C - 1, (ob, B_mT[kc]), (PS[bk],))
                    DVE(lambda e, c=c, bk=bk: e.scalar_tensor_tensor(
                        xT[:, c, :nt], ps[:, bk, :nt], G[:, c:c + 1], xT[:, c, :nt], ALU.mult, ALU.add),
                        (PS[bk], B_xT[c], B_dv), (B_xT[c],))

        def final_out(nt, tile):
            rms_stats(nt)
            switch_big(B_ostage)
            for c in range(C):
                DVE(lambda e, c=c: e.scalar_tensor_tensor(
                    xT[:, c, :nt], xT[:, c, :nt], final_g[:, c:c + 1], rstd[:, :nt], ALU.mult, ALU.mult),
                    (B_xT[c], B_rstd, B_const), (B_xT[c],))
            dst = yp if tile == 0 else ys
            r0 = 0 if tile == 0 else (tile - 1) * 512
            n = 0
            for sub in range(nt // 128):
                for c4 in range(4):
                    bank = n % 4
                    n += 1
                    for q in range(4):
                        c = c4 * 4 + q
                        PE(lambda e, bank=bank, q=q, c=c, sub=sub: e.transpose(
                            ps[:, bank, q * 128:(q + 1) * 128], xT[:, c, sub * 128:(sub + 1) * 128], ident[:]),
                            (B_xT[c], B_const), (PS[bank],))
                    o_ap = ostage[:, sub % 2, c4 * 512:(c4 + 1) * 512]
                    if c4 % 2 == 0:
                        ACT(lambda e, o_ap=o_ap, bank=bank: e.activation(o_ap, ps[:, bank, :], AF.Copy),
                            (PS[bank],), (B_ostage[sub % 2],))
                    else:
                        DVE(lambda e, o_ap=o_ap, bank=bank: e.tensor_copy(o_ap, ps[:, bank, :]),
                            (PS[bank],), (B_ostage[sub % 2],))
                SPD(dst[r0 + sub * 128:r0 + (sub + 1) * 128, :], ostage[:, sub % 2, :], (B_ostage[sub % 2],), ())

        def sset(tile):
            return 0 if tile == 0 else 1

        st = stages or {}
        if dbg:
            dbg_dv = dout("dbg_dv", [128, DEPTH * 2 * 9 * C])
            SPD(dbg_dv, dv[:].rearrange("p a b c d -> p (a b c d)"), (B_dv,), ())
        parts = st.get("parts", "xfsi")
        for tile in st.get("p0", range(6)):
            if "x" in parts:
                load_x_input(tile)
            if "f" in parts:
                ffn(512, 0, 0, sset(tile))
            if "s" in parts:
                store_x1(0, tile, 512)
            if "i" in parts:
                in_proj1(512, 0, sset(tile), tile)
        for tile in st.get("p1", range(6)):
            nt = 256 if tile == 5 else 512
            load_x1(0, tile, nt)
            conv_branch(nt, 0, tile)
            attention(nt, 0, tile)
            merge_out(nt, 0, sset(tile))
            ffn(nt, 0, 1, sset(tile))
            ffn(nt, 1, 0, sset(tile))
            store_x1(1, tile, nt)
            in_proj1(nt, 1, sset(tile), tile)
        for tile in st.get("p2", range(5)):
            load_x1(1, tile, 512)
            conv_branch(512, 1, tile)
            attention(512, 1, tile)
            merge_out(512, 1, sset(tile))
            ffn(512, 1, 1, sset(tile))
            final_out(512, tile)

        emit_program(nc, P, esem, dsem, block)
    nc._k_declared = list(declared)
    return nc


def _fm(v):
    v = np.asarray(v, np.float32)
    lead = v.shape[:-1]
    n = v.shape[-1] // 128
    v = v.reshape(lead + (n, 128))
    return np.ascontiguousarray(np.moveaxis(v, -1, 0))


def _bias_tables(rel_bias, half):
    def table(m):
        cm0 = max(m - 2, 0)
        key = np.arange(128)
        qry = np.arange(128)
        ch = np.arange(5)
        rk_l = 2 * (cm0 + ch)[None, :, None] + (key // 64)[:, None, None]
        ck_l = (key % 64)[:, None, None] + 0 * ch[None, :, None]
        rq_l = (2 * m + qry // 64)[None, None, :]
        cq_l = (qry % 64)[None, None, :]
        if half == 0:
            rk, ckk, rq, cq = rk_l, ck_l, rq_l, cq_l
        else:
            rk, ckk, rq, cq = 63 - rk_l, 63 - ck_l, 63 - rq_l, 63 - cq_l
        rs = np.clip(rq - 4, 0, 56)
        cs = np.clip(cq - 8, 0, 48)
        valid = (rk >= rs) & (rk <= rs + 7) & (ckk >= cs) & (ckk <= cs + 15)
        dr = np.clip(rk - rq + 7, 0, 14)
        dc = np.clip(ckk - cq + 15, 0, 30)
        dr, dc, valid = np.broadcast_arrays(dr, dc, valid)
        g = rel_bias[:, :, dr, dc]
        g = np.where(valid[None, None], g, np.float32(NEG)).astype(np.float32)
        return g.reshape(DEPTH, NH, 128, 640)
    first = np.stack([table(m) for m in range(4)], axis=2)
    gen = table(8)
    return np.ascontiguousarray(first), np.ascontiguousarray(gen)


_NC_CACHE = {}


def _get_nc(dbg=False):
    if dbg not in _NC_CACHE:
        _NC_CACHE[dbg] = build_program(dbg)
    return _NC_CACHE[dbg]


def make_in_maps(x_prompt, x_sample, cache_k, cache_v, c, c_ctx, w_ada, b_ada, norm_g,
                 ffn_w_gate, ffn_w_up, ffn_w_down, w_in, conv_w, conv_b, conv_ln_g,
                 conv_ln_b, w_conv_out, rel_bias, w_na_out, w_out, final_g):
    f = lambda a: np.ascontiguousarray(np.asarray(a, np.float32))
    x_prompt, x_sample, cache_k, cache_v = f(x_prompt), f(x_sample), f(cache_k), f(cache_v)
    c, c_ctx, rel_bias, conv_w = f(c), f(c_ctx), f(rel_bias), f(conv_w)
    shared = {
        "w_ada": f(w_ada), "wg": f(ffn_w_gate), "wu": f(ffn_w_up), "wd": f(ffn_w_down),
        "w_in": f(w_in), "w_co": f(w_conv_out), "w_no": f(w_na_out), "w_o": f(w_out),
        "b_ada_t": _fm(b_ada), "norm_g_t": _fm(norm_g), "final_g_t": _fm(final_g),
        "convv_t": _fm(np.stack([f(conv_b), f(conv_ln_g), f(conv_ln_b)], 0)),
        "ident": np.eye(128, dtype=np.float32),
    }
    def taps(cw):
        t = np.transpose(cw, (0, 2, 1)).reshape(DEPTH, 8, 128, 31)
        return np.ascontiguousarray(np.transpose(t, (2, 0, 1, 3)))
    tp_nat = taps(conv_w)
    tp_rev = taps(conv_w[:, ::-1, :])
    tables = [_bias_tables(rel_bias, 0), _bias_tables(rel_bias, 1)]
    in_maps = []
    for i in range(8):
        b, half = i // 2, i % 2
        if half == 0:
            xs = x_sample[b, 0:2560]
            cws = tp_nat
        else:
            xs = x_sample[b, ::-1][0:2560]
            cws = tp_rev
        m = dict(shared)
        m["xp"] = np.ascontiguousarray(x_prompt[2 * i:2 * i + 2].reshape(512, D))
        m["xs"] = np.ascontiguousarray(xs)
        m["ck"] = np.ascontiguousarray(cache_k[b].reshape(DEPTH, 512, 1024))
        m["cv"] = np.ascontiguousarray(cache_v[b].reshape(DEPTH, 512, 1024))
        m["cvec"] = np.ascontiguousarray(np.stack([_fm(c_ctx), _fm(c[b])], axis=-1))
        m["convw_t"] = np.ascontiguousarray(np.stack([tp_nat, cws], axis=1))
        m["bias_first"], m["bias_gen"] = tables[half]
        in_maps.append(m)
    return in_maps


def assemble(results):
    y_p = np.zeros((16, 256, D), np.float32)
    y_s = np.zeros((4, 4096, D), np.float32)
    n_k = np.zeros((16, DEPTH, 256, NH, 64), np.float32)
    n_v = np.zeros((16, DEPTH, 256, NH, 64), np.float32)
    for i, r in enumerate(results):
        b, half = i // 2, i % 2
        y_p[2 * i:2 * i + 2] = np.asarray(r["yp"]).reshape(2, 256, D)
        ysl = np.asarray(r["ys"])
        if half == 0:
            y_s[b, 0:2048] = ysl
        else:
            y_s[b, 2048:4096] = ysl[::-1]
        n_k[2 * i:2 * i + 2] = np.asarray(r["nk"]).reshape(2, DEPTH, 256, NH, 64)
        n_v[2 * i:2 * i + 2] = np.asarray(r["nv"]).reshape(2, DEPTH, 256, NH, 64)
    return y_p, y_s, n_k, n_v


def kernel(**inputs):
    nc = _get_nc(False)
    in_maps = make_in_maps(**inputs)
    names = set(nc._k_declared)
    in_maps = [{k: v for k, v in m.items() if k in names} for m in in_maps]
    res = run_bass_kernel_spmd(nc, in_maps, core_ids=list(range(8)))
    return assemble(res.results)
```

```python
import contextlib
import numpy as np
import concourse.bass as bass
import concourse.mybir as mybir
from concourse.bass_utils import run_bass_kernel_spmd

F32 = mybir.dt.float32
BF16 = mybir.dt.bfloat16
AF = mybir.ActivationFunctionType
ALU = mybir.AluOpType

D = 2048
C = 16
DFF = 5504
NF = 43
DEPTH = 2
NH = 16
EPS = 1e-6
NEG = -30000.0
NTOK = 3072
RING_SLOTS = 4
RING_W = 8192

ENGS = ("pe", "act", "dve", "pool", "sp")
DMA_K = {"pool": 8, "sp": 16}


class Buf:
    __slots__ = ("name", "w", "r", "rd", "excl")

    def __init__(self, name, excl=False):
        self.name = name
        self.excl = excl
        self.w = None
        self.r = {}
        self.rd = []


class Ins:
    __slots__ = ("eng", "fn", "deps", "flag", "val", "dma", "qi", "seq")


class Prog:
    def __init__(self):
        self.q = {e: [] for e in ENGS}
        self.ndma = {e: 0 for e in ENGS}
        self.nseq = 0
        self.region_latest = {}

    def add(self, eng, fn, reads=(), writes=(), dma=False):
        ins = Ins()
        ins.eng = eng
        ins.fn = fn
        ins.dma = dma
        ins.flag = dma
        ins.val = None
        ins.qi = None
        ins.seq = self.nseq
        self.nseq += 1
        if dma:
            ins.qi = self.ndma[eng]
            self.ndma[eng] += 1
        deps = []
        for b in reads:
            if b.w is not None:
                deps.append(b.w)
            if b.excl:
                deps.extend(r for e2, r in b.r.items() if e2 != eng)
        for b in writes:
            if b.w is not None:
                deps.append(b.w)
            deps.extend(b.r.values())
            deps.extend(b.rd)
        dd = []
        seen = set()
        for d in deps:
            if d is ins or id(d) in seen:
                continue
            seen.add(id(d))
            if (not d.dma) and (not dma) and d.eng == "pe" and eng == "pe":
                continue
            d.flag = True
            dd.append(d)
        ins.deps = dd
        for b in reads:
            if dma:
                b.rd.append(ins)
            else:
                b.r[eng] = ins
        for b in writes:
            b.w = ins
            b.r = {}
            b.rd = []
        self.q[eng].append(ins)
        return ins

    def handoff(self, olds, news):
        keep = set(id(b) for b in olds) & set(id(b) for b in news)
        olds = [b for b in olds if id(b) not in keep]
        news = [b for b in news if id(b) not in keep]
        pend = []
        for b in olds:
            if b.w is not None:
                pend.append(b.w)
            pend.extend(b.r.values())
            pend.extend(b.rd)
            b.w = None
            b.r = {}
            b.rd = []
        latest = dict(self.region_latest)
        dmas = []
        for a in pend:
            if a.dma:
                dmas.append(a)
            elif a.eng not in latest or a.seq > latest[a.eng].seq:
                latest[a.eng] = a
        self.region_latest = latest
        pend = list(latest.values()) + dmas
        for b in news:
            b.w = None
            b.r = {}
            b.rd = list(pend)


def emit_program(nc, P, esem, dsem, block):
    for e in ENGS:
        cnt = 0
        for ins in P.q[e]:
            if ins.dma:
                continue
            if ins.flag:
                cnt += 1
                ins.val = cnt

    def event(d):
        if d.dma:
            k = DMA_K[d.eng]
            return dsem[d.eng][d.qi % k], 16 * (d.qi // k + 1)
        return esem[d.eng], d.val

    def run(eng_name, e):
        waited = {}

        def wait(sem, val):
            key = id(sem)
            if waited.get(key, 0) >= val:
                return
            waited[key] = val
            e.wait_ge(sem, val)

        for ins in P.q[eng_name]:
            for d in ins.deps:
                s, v = event(d)
                wait(s, v)
            if ins.dma:
                k = DMA_K[eng_name]
                if ins.qi >= k:
                    wait(dsem[eng_name][ins.qi % k], 16 * (ins.qi // k))
                bi = ins.fn(e)
                bi.then_inc(dsem[eng_name][ins.qi % k], 16)
            else:
                bi = ins.fn(e)
                if ins.flag:
                    bi.then_inc(esem[eng_name], 1)
        if eng_name in DMA_K:
            k = DMA_K[eng_name]
            n = P.ndma[eng_name]
            for j in range(min(k, n)):
                cntj = len(range(j, n, k))
                wait(dsem[eng_name][j], 16 * cntj)

    @block.tensor
    def _(e):
        run("pe", e)

    @block.scalar
    def _(e):
        run("act", e)

    @block.vector
    def _(e):
        run("dve", e)

    @block.gpsimd
    def _(e):
        run("pool", e)

    @block.sync
    def _(e):
        run("sp", e)


def build_program(dbg=False, stages=None):
    nc = bass.Bass("TRN2", target_bir_lowering=False)
    P = Prog()
    nc._k_declared = None

    declared = []

    class LazyIn:
        def __init__(self, name, shape, dt):
            self.name, self.shape, self.dt, self._ap = name, list(shape), dt, None

        def get(self):
            if self._ap is None:
                self._ap = nc.dram_tensor(self.name, self.shape, self.dt, kind="ExternalInput").ap()
                declared.append(self.name)
            return self._ap

        def __getitem__(self, k):
            return self.get()[k]

        def rearrange(self, *a, **k):
            return self.get().rearrange(*a, **k)

    def din(name, shape, dt=F32):
        return LazyIn(name, shape, dt)

    def dout(name, shape, dt=F32):
        return nc.dram_tensor(name, list(shape), dt, kind="ExternalOutput").ap()

    def dscr(name, shape, dt):
        return nc.dram_tensor(name, list(shape), dt,
                              kind=("ExternalOutput" if dbg else "Internal")).ap()

    xp = din("xp", [512, D])
    xs = din("xs", [2560, D])
    ck_d = din("ck", [DEPTH, 512, 1024])
    cv_d = din("cv", [DEPTH, 512, 1024])
    cvec_d = din("cvec", [128, C, 2])
    w_ada = din("w_ada", [DEPTH, D, 9 * D])
    b_ada_d = din("b_ada_t", [128, DEPTH, 144])
    norm_g_d = din("norm_g_t", [128, DEPTH, 3, C])
    final_g_d = din("final_g_t", [128, C])
    wg = din("wg", [DEPTH, 2, D, DFF])
    wu = din("wu", [DEPTH, 2, D, DFF])
    wd = din("wd", [DEPTH, 2, DFF, D])
    w_in = din("w_in", [DEPTH, D, 9216])
    convw_d = din("convw_t", [128, 2, DEPTH, 8, 31])
    convv_d = din("convv_t", [128, 3, DEPTH, 8])
    w_co = din("w_co", [DEPTH, 1024, D])
    w_no = din("w_no", [DEPTH, 1024, D])
    w_o = din("w_o", [DEPTH, D, D])
    bias_first = din("bias_first", [DEPTH, NH, 4, 128, 640])
    bias_gen = din("bias_gen", [DEPTH, NH, 128, 640])
    ident_d = din("ident", [128, 128])

    yp = dout("yp", [512, D])
    ys = dout("ys", [2048, D])
    nk = dout("nk", [2, DEPTH, 256, 1024])
    nv = dout("nv", [2, DEPTH, 256, 1024])

    x1T_s = [dscr(f"x1T{l}", [C, 128, NTOK], F32) for l in range(DEPTH)]
    uT_s = [dscr(f"uT{l}", [C, 128, NTOK], BF16) for l in range(DEPTH)]
    hcT_s = [dscr(f"hcT{l}", [8, 128, NTOK], BF16) for l in range(DEPTH)]
    qT_s = [dscr(f"qT{l}", [8, 128, NTOK], BF16) for l in range(DEPTH)]
    kT_s = [dscr(f"kT{l}", [8, 128, NTOK], BF16) for l in range(DEPTH)]
    vt_s = [dscr(f"vt{l}", [NTOK, 1024], BF16) for l in range(DEPTH)]
    ckT_s = [dscr(f"ckT{l}", [8, 128, 512], BF16) for l in range(DEPTH)]
    cvb_s = [dscr(f"cvb{l}", [8, 128, 4, 128], BF16) for l in range(DEPTH)]

    def tilebufs(name):
        return [[Buf(f"{name}{l}_{t}") for t in range(NTOK // 512)] for l in range(DEPTH)]

    B_x1T = tilebufs("x1T")
    B_uT = tilebufs("uT")
    B_hcT = tilebufs("hcT")
    B_qT = tilebufs("qT")
    B_kT = tilebufs("kT")
    B_vt = tilebufs("vt")
    B_ckT = [Buf(f"ckT{l}") for l in range(DEPTH)]
    B_cvb = [Buf(f"cvb{l}") for l in range(DEPTH)]

    es = contextlib.ExitStack()
    with es:
        def sb(name, shape, dt):
            return es.enter_context(nc.sbuf_tensor("sb_" + name, list(shape), dt))

        ring = sb("ring", [128, RING_SLOTS, RING_W], BF16)
        xT = sb("xT", [128, C, 512], F32)
        hT = sb("hT", [128, C, 512], BF16)
        big = sb("big", [128, 36608], BF16)
        ident = sb("ident", [128, 128], F32)
        ones_bf = sb("ones_bf", [128, 128], BF16)
        scv = sb("scv", [128, C, 2], BF16)
        cvec = sb("cvec", [128, C, 2], F32)
        b_ada = sb("b_ada", [128, DEPTH, 144], F32)
        norm_g = sb("norm_g", [128, DEPTH, 3, C], F32)
        final_g = sb("final_g", [128, C], F32)
        convw = sb("convw", [128, 2, DEPTH, 8, 31], F32)
        convv = sb("convv", [128, 3, DEPTH, 8], F32)
        modv = sb("modv", [128, DEPTH, 2, 144], F32)
        dv = sb("dv", [128, DEPTH, 2, 9, C], F32)
        sq = sb("sq", [128, 2, 512], BF16)
        stdt = sb("stdt", [128, 512], F32)
        rstd = sb("rstd", [128, 512], F32)
        meant = sb("meant", [128, 512], F32)
        tmpf = sb("tmpf", [128, 2, 512], F32)
        ps = es.enter_context(nc.psum_tensor("ps", [128, 8, 512], F32))
        ps_flat = ps[:].rearrange("p a b -> p (a b)")

        esem = {e: es.enter_context(nc.semaphore(f"e_{e}")) for e in ("pe", "act", "dve")}
        dsem = {q: [es.enter_context(nc.semaphore(f"d_{q}{i}")) for i in range(k)]
                for q, k in DMA_K.items()}
        block = es.enter_context(nc.Block())

        PS = [Buf(f"ps{i}", excl=True) for i in range(8)]
        B_ring = [Buf(f"ring{i}") for i in range(RING_SLOTS)]
        B_xT = [Buf(f"xT{c}") for c in range(C)]
        B_hT = Buf("hT")
        B_sq = [Buf("sq0"), Buf("sq1")]
        B_std = Buf("std")
        B_rstd = Buf("rstd")
        B_mean = Buf("mean")
        B_tmpf = [Buf("tmpf0"), Buf("tmpf1")]
        B_const = Buf("const")
        B_dv = Buf("dv")
        B_modv = Buf("modv")
        B_scv = Buf("scv")

        def PE(fn, r=(), w=()):
            return P.add("pe", fn, r, w)

        def ACT(fn, r=(), w=()):
            return P.add("act", fn, r, w)

        def DVE(fn, r=(), w=()):
            return P.add("dve", fn, r, w)

        def SPD(out, in_, r=(), w=()):
            if isinstance(in_, LazyIn):
                in_ = in_.get()
            return P.add("sp", lambda e, o=out, i=in_: e.dma_start(out=o, in_=i), r, w, dma=True)

        def mm(out, lhsT, rhs, start, stop, r, w):
            return PE(lambda e, o=out, a=lhsT, b=rhs, s=start, t=stop:
                      e.matmul(o, a, b, start=s, stop=t), r, w)

        ring_n = [0]

        def ring_load(src, k, n):
            i = ring_n[0] % RING_SLOTS
            ring_n[0] += 1
            view = ring[:, i, 0:k * n].rearrange("p (k n) -> p k n", n=n)
            P.add("pool", lambda e, o=view, s=src: e.dma_start(out=o, in_=s),
                  (), (B_ring[i],), dma=True)
            return view, B_ring[i]

        def bview(a, b):
            return big[:, a:b]

        aT = bview(0, 22016).rearrange("p (f t) -> p f t", t=512)
        sgt = bview(22016, 24064).bitcast(F32).rearrange("p (a t) -> p a t", t=512)
        xstage = bview(24064, 32256).bitcast(F32).rearrange("p (a t) -> p a t", t=2048)
        B_aT = [Buf(f"aT{f}") for f in range(NF)]
        B_sgt = [Buf("sgt0"), Buf("sgt1")]
        B_xstage = [Buf("xst0"), Buf("xst1")]
        FFN_BUFS = B_aT + B_sgt + B_xstage
        st_bf = bview(0, 4096).rearrange("p (a t) -> p a t", t=512)
        st_f32 = bview(4096, 8192).bitcast(F32).rearrange("p (a t) -> p a t", t=512)
        B_stbf = [Buf(f"stbf{i}") for i in range(8)]
        B_stf = [Buf(f"stf{i}") for i in range(4)]
        IP_BUFS = B_stbf + B_stf
        convh = bview(0, 4096).rearrange("p (a t) -> p a t", t=512)
        naoT = bview(4096, 8192).rearrange("p (a t) -> p a t", t=512)
        Y0 = 8192
        convacc = bview(Y0, Y0 + 8192).bitcast(F32).rearrange("p (a t) -> p a t", t=512)
        hcbuf = bview(Y0 + 8192, Y0 + 8192 + 4608).rearrange("p (a t) -> p a t", t=576)
        B_convh = [Buf(f"convh{i}") for i in range(8)]
        B_naoT = [Buf(f"naoT{i}") for i in range(8)]
        B_convacc = [Buf(f"convacc{i}") for i in range(8)]
        B_hcbuf = Buf("hcbuf")
        CONV_BUFS = B_convacc + [B_hcbuf]
        o = Y0 + 12800
        kb = bview(o, o + 2048).rearrange("p (a t) -> p a t", t=1024); o += 2048
        vb = bview(o, o + 2048).rearrange("p (a c d) -> p a c d", c=8, d=128); o += 2048
        qb = bview(o, o + 1024).rearrange("p (a t) -> p a t", t=512); o += 1024
        ckb = bview(o, o + 1024).rearrange("p (a t) -> p a t", t=512); o += 1024
        cvbuf = bview(o, o + 1024).rearrange("p (a c d) -> p a c d", c=4, d=128); o += 1024
        biasb = bview(o, o + 2560).bitcast(F32).rearrange("p (a t) -> p a t", t=640); o += 2560
        sbb = bview(o, o + 2560).bitcast(F32).rearrange("p (a t) -> p a t", t=640); o += 2560
        pb = bview(o, o + 1280).rearrange("p (a t) -> p a t", t=640); o += 1280
        pc = bview(o, o + 1024).rearrange("p (a t) -> p a t", t=512); o += 1024
        rcp = bview(o, o + 1024).bitcast(F32); o += 1024
        assert o <= 36608
        B_kvq = [Buf("kvq0"), Buf("kvq1")]
        B_biasb = [Buf("biasb0"), Buf("biasb1")]
        B_sbb = [Buf("sbb0"), Buf("sbb1")]
        B_pb = [Buf("pb0"), Buf("pb1")]
        B_pc = [Buf("pc0"), Buf("pc1")]
        B_rcp = Buf("rcp")
        ATT_BUFS = B_kvq + B_biasb + B_sbb + B_pb + B_pc + [B_rcp]
        t1 = bview(Y0, Y0 + 4096).bitcast(F32).rearrange("p (a t) -> p a t", t=512)
        mT = bview(Y0 + 4096, Y0 + 4096 + 8192).rearrange("p (a t) -> p a t", t=512)
        B_t1 = [Buf(f"t1_{i}") for i in range(4)]
        B_mT = [Buf(f"mT{i}") for i in range(C)]
        MRG_BUFS = B_t1 + B_mT
        ostage = bview(0, 8192).bitcast(F32).rearrange("p (a t) -> p a t", t=2048)
        B_ostage = [Buf("ost0"), Buf("ost1")]
        cstage = bview(0, 8192).bitcast(F32).rearrange("p (c t) -> p c t", t=1024)
        cstb = bview(8192, 12288).rearrange("p (c t) -> p c t", t=1024)
        ckTt = bview(12288, 16384).rearrange("p (c t) -> p c t", t=512)
        B_cstage = Buf("cstage")
        B_cstb = Buf("cstb")
        B_ckTt = Buf("ckTt")

        cur_big = [[]]

        def switch_big(news):
            P.handoff(cur_big[0], news)
            cur_big[0] = list(news)

        SPD(ident[:], ident_d, (), (B_const,))
        SPD(cvec[:], cvec_d, (), (B_const,))
        SPD(b_ada[:], b_ada_d, (), (B_const,))
        SPD(norm_g[:], norm_g_d, (), (B_const,))
        SPD(final_g[:], final_g_d, (), (B_const,))
        SPD(convw[:], convw_d, (), (B_const,))
        SPD(convv[:], convv_d, (), (B_const,))
        DVE(lambda e: e.memset(ones_bf[:], 1.0), (), (B_const,))
        ACT(lambda e: e.activation(scv[:], cvec[:], AF.Silu), (B_const,), (B_scv,))

        st = stages or {}
        psm = ps[:, 0, 0:288]
        for l in (range(DEPTH) if st.get("ada", True) else []):
            wv = w_ada[l].rearrange("(c p) n -> p c n", p=128)
            for blk in range(36):
                view, rb = ring_load(wv[:, :, blk * 512:(blk + 1) * 512], C, 512)
                for j in range(4):
                    col = (blk * 4 + j) * 2
                    for kc in range(C):
                        mm(psm[:, col:col + 2], view[:, kc, j * 128:(j + 1) * 128], scv[:, kc, :],
                           kc == 0, kc == C - 1, (rb, B_scv), (PS[0],))
            psm3 = psm.rearrange("p (j s) -> p j s", s=2)
            for s in range(2):
                DVE(lambda e, l=l, s=s, psm3=psm3: e.tensor_tensor(
                    modv[:, l, s, :], psm3[:, :, s], b_ada[:, l, :], ALU.add),
                    (PS[0], B_const), (B_modv,))
            for s in range(2):
                for k3 in range(3):
                    shift = modv[:, l, s, (3 * k3) * C:(3 * k3 + 1) * C]
                    scale = modv[:, l, s, (3 * k3 + 1) * C:(3 * k3 + 2) * C]
                    gate = modv[:, l, s, (3 * k3 + 2) * C:(3 * k3 + 3) * C]
                    DVE(lambda e, l=l, s=s, k3=k3, scale=scale: e.scalar_tensor_tensor(
                        dv[:, l, s, 3 * k3, :], scale, 1.0, norm_g[:, l, k3, :], ALU.add, ALU.mult),
                        (B_modv, B_const), (B_dv,))
                    DVE(lambda e, l=l, s=s, k3=k3, shift=shift: e.tensor_copy(
                        dv[:, l, s, 3 * k3 + 1, :], shift), (B_modv,), (B_dv,))
                    gsc = 1.0 if k3 == 1 else 0.5
                    DVE(lambda e, l=l, s=s, k3=k3, gate=gate, gsc=gsc: e.tensor_scalar(
                        dv[:, l, s, 3 * k3 + 2, :], gate, gsc, None, ALU.mult),
                        (B_modv,), (B_dv,))

        switch_big([B_cstage, B_cstb, B_ckTt])
        for l in (range(DEPTH) if st.get("cache", True) else []):
            SPD(cstage, ck_d[l].rearrange("(c p) n -> p c n", p=128), (), (B_cstage,))
            n = 0
            for fc in range(8):
                for tc in range(4):
                    bank = 4 + (n // 4) % 4
                    sl = n % 4
                    PE(lambda e, bank=bank, sl=sl, tc=tc, fc=fc: e.transpose(
                        ps[:, bank, sl * 128:(sl + 1) * 128], cstage[:, tc, fc * 128:(fc + 1) * 128],
                        ident[:]), (B_cstage, B_const), (PS[bank],))
                    n += 1
                    if sl == 3:
                        ACT(lambda e, bank=bank, fc=fc: e.activation(
                            ckTt[:, fc, :], ps[:, bank, :], AF.Copy), (PS[bank],), (B_ckTt,))
            SPD(ckT_s[l].rearrange("c p t -> p c t"), ckTt, (B_ckTt,), (B_ckT[l],))
            SPD(cstage, cv_d[l].rearrange("(c p) n -> p c n", p=128), (), (B_cstage,))
            DVE(lambda e: e.tensor_copy(cstb, cstage), (B_cstage,), (B_cstb,))
            SPD(cvb_s[l].rearrange("j p c d -> p c j d"),
                cstb.rearrange("p c (j d) -> p c j d", d=128), (B_cstb,), (B_cvb[l],))

        def rms_stats(nt):
            for c in range(C):
                DVE(lambda e, c=c: e.tensor_tensor(sq[:, c % 2, :nt], xT[:, c, :nt], xT[:, c, :nt],
                                                   ALU.mult), (B_xT[c],), (B_sq[c % 2],))
                mm(ps[:, 7, :nt], ones_bf[:], sq[:, c % 2, :nt], c == 0, c == C - 1,
                   (B_sq[c % 2], B_const), (PS[7],))
            ACT(lambda e: e.activation(stdt[:, :nt], ps[:, 7, :nt], AF.Sqrt, bias=EPS, scale=1.0 / D),
                (PS[7],), (B_std,))
            DVE(lambda e: e.reciprocal(rstd[:, :nt], stdt[:, :nt]), (B_std,), (B_rstd,))

        def modulate(nt, l, s, k3):
            A = dv[:, l, s, 3 * k3, :]
            Bv = dv[:, l, s, 3 * k3 + 1, :]
            for c in range(C):
                DVE(lambda e, c=c, A=A: e.scalar_tensor_tensor(
                    tmpf[:, c % 2, :nt], xT[:, c, :nt], A[:, c:c + 1], rstd[:, :nt], ALU.mult, ALU.mult),
                    (B_xT[c], B_rstd, B_dv), (B_tmpf[c % 2],))
                ACT(lambda e, c=c, Bv=Bv: e.activation(
                    hT[:, c, :nt], tmpf[:, c % 2, :nt], AF.Identity, bias=Bv[:, c:c + 1]),
                    (B_tmpf[c % 2], B_dv), (B_hT,))

        def ffn(nt, l, w, s):
            k3 = 0 if w == 0 else 2
            switch_big(FFN_BUFS)
            rms_stats(nt)
            modulate(nt, l, s, k3)
            wgv = wg[l, w].rearrange("(c p) n -> p c n", p=128)
            wuv = wu[l, w].rearrange("(c p) n -> p c n", p=128)
            wdv = wd[l, w].rearrange("(f p) n -> p f n", p=128)
            G = dv[:, l, s, 3 * k3 + 2, :]
            for blk in range(11):
                c0 = blk * 512
                wb = min(512, DFF - c0)
                gv, gb = ring_load(wgv[:, :, c0:c0 + wb], C, wb)
                uv, ub = ring_load(wuv[:, :, c0:c0 + wb], C, wb)
                for j in range(wb // 128):
                    f = blk * 4 + j
                    bg = (f % 2) * 2
                    bu = bg + 1
                    for kc in range(C):
                        mm(ps[:, bg, :nt], gv[:, kc, j * 128:(j + 1) * 128], hT[:, kc, :nt],
                           kc == 0, kc == C - 1, (gb, B_hT), (PS[bg],))
                    for kc in range(C):
                        mm(ps[:, bu, :nt], uv[:, kc, j * 128:(j + 1) * 128], hT[:, kc, :nt],
                           kc == 0, kc == C - 1, (ub, B_hT), (PS[bu],))
                    ACT(lambda e, f=f, bg=bg: e.activation(sgt[:, f % 2, :nt], ps[:, bg, :nt], AF.Silu),
                        (PS[bg],), (B_sgt[f % 2],))
                    DVE(lambda e, f=f, bu=bu: e.tensor_tensor(
                        aT[:, f, :nt], sgt[:, f % 2, :nt], ps[:, bu, :nt], ALU.mult),
                        (B_sgt[f % 2], PS[bu]), (B_aT[f],))
            for dp in range(8):
                banks = (4, 5) if dp % 2 == 0 else (6, 7)
                for half in range(2):
                    f0, f1 = (0, 22) if half == 0 else (22, NF)
                    dvw, db = ring_load(wdv[:, f0:f1, dp * 256:(dp + 1) * 256], f1 - f0, 256)
                    for dj in range(2):
                        for f in range(f0, f1):
                            mm(ps[:, banks[dj], :nt], dvw[:, f - f0, dj * 128:(dj + 1) * 128],
                               aT[:, f, :nt], f == 0, f == NF - 1, (db, B_aT[f]), (PS[banks[dj]],))
                for dj in range(2):
                    c = dp * 2 + dj
                    DVE(lambda e, c=c, bk=banks[dj], G=G: e.scalar_tensor_tensor(
                        xT[:, c, :nt], ps[:, bk, :nt], G[:, c:c + 1], xT[:, c, :nt], ALU.mult, ALU.add),
                        (PS[banks[dj]], B_xT[c], B_dv), (B_xT[c],))

        def load_x_input(tile):
            switch_big(FFN_BUFS)
            src = xp if tile == 0 else xs
            r0 = 0 if tile == 0 else (tile - 1) * 512
            n = 0
            for sub in range(4):
                SPD(xstage[:, sub % 2, :], src[r0 + sub * 128: r0 + (sub + 1) * 128, :],
                    (), (B_xstage[sub % 2],))
                for c4 in range(4):
                    bank = 4 + n % 4
                    n += 1
                    for q in range(4):
                        c = c4 * 4 + q
                        PE(lambda e, bank=bank, q=q, c=c, sub=sub: e.transpose(
                            ps[:, bank, q * 128:(q + 1) * 128], xstage[:, sub % 2, c * 128:(c + 1) * 128],
                            ident[:]), (B_xstage[sub % 2], B_const), (PS[bank],))
                    eng = ACT if c4 % 2 == 0 else DVE
                    dst = xT[:, c4 * 4:(c4 + 1) * 4, sub * 128:(sub + 1) * 128]
                    srcp = ps[:, bank, :].rearrange("p (q t) -> p q t", t=128)
                    if c4 % 2 == 0:
                        ACT(lambda e, dst=dst, srcp=srcp: e.activation(dst, srcp, AF.Copy),
                            (PS[bank],), tuple(B_xT[c4 * 4:(c4 + 1) * 4]))
                    else:
                        DVE(lambda e, dst=dst, srcp=srcp: e.tensor_copy(dst, srcp),
                            (PS[bank],), tuple(B_xT[c4 * 4:(c4 + 1) * 4]))

        def store_x1(l, tile, nt):
            g0 = tile * 512
            SPD(x1T_s[l].rearrange("c p t -> p c t")[:, :, g0:g0 + nt], xT[:, :, :nt],
                tuple(B_xT), (B_x1T[l][tile],))

        def load_x1(l, tile, nt):
            g0 = tile * 512
            SPD(xT[:, :, :nt], x1T_s[l].rearrange("c p t -> p c t")[:, :, g0:g0 + nt],
                (B_x1T[l][tile],), tuple(B_xT))
            SPD(hT[:, :, :nt], uT_s[l].rearrange("c p t -> p c t")[:, :, g0:g0 + nt],
                (B_uT[l][tile],), (B_hT,))

        def in_proj1(nt, l, s, tile):
            is_p = tile == 0 and not st.get("noprompt", False)
            g0 = tile * 512
            rms_stats(nt)
            modulate(nt, l, s, 1)
            switch_big(IP_BUFS)
            SPD(uT_s[l].rearrange("c p t -> p c t")[:, :, g0:g0 + nt], hT[:, :, :nt],
                (B_hT,), (B_uT[l][tile],))
            wv = w_in[l].rearrange("(c p) n -> p c n", p=128)
            nst = [0]
            nsf = [0]
            npb = [0]

            def stb():
                i = nst[0] % 8
                nst[0] += 1
                return st_bf[:, i, :], B_stbf[i]

            def stf():
                i = nsf[0] % 4
                nsf[0] += 1
                return st_f32[:, i, :], B_stf[i]

            def pbank(lo, n):
                i = lo + npb[0] % n
                npb[0] += 1
                return i

            for half in range(2):
                av, ab = ring_load(wv[:, :, half * 512:(half + 1) * 512], C, 512)
                gvw, gbb = ring_load(wv[:, :, 1024 + half * 512:1024 + (half + 1) * 512], C, 512)
                for j in range(4):
                    ch = half * 4 + j
                    ba = (ch % 2) * 2
                    bgk = ba + 1
                    for kc in range(C):
                        mm(ps[:, ba, :nt], av[:, kc, j * 128:(j + 1) * 128], hT[:, kc, :nt],
                           kc == 0, kc == C - 1, (ab, B_hT), (PS[ba],))
                    for kc in range(C):
                        mm(ps[:, bgk, :nt], gvw[:, kc, j * 128:(j + 1) * 128], hT[:, kc, :nt],
                           kc == 0, kc == C - 1, (gbb, B_hT), (PS[bgk],))
                    sv, sbuf_ = stf()
                    ACT(lambda e, sv=sv, bgk=bgk: e.activation(sv[:, :nt], ps[:, bgk, :nt], AF.Sigmoid),
                        (PS[bgk],), (sbuf_,))
                    ov, obuf = stb()
                    DVE(lambda e, ov=ov, sv=sv, ba=ba: e.tensor_tensor(
                        ov[:, :nt], sv[:, :nt], ps[:, ba, :nt], ALU.mult), (sbuf_, PS[ba]), (obuf,))
                    SPD(hcT_s[l][ch, :, g0:g0 + nt], ov[:, :nt], (obuf,), (B_hcT[l][tile],))
            for which in range(2):
                dst_s = qT_s[l] if which == 0 else kT_s[l]
                dst_b = B_qT[l][tile] if which == 0 else B_kT[l][tile]
                for half in range(2):
                    cb0 = 2048 + which * 1024 + half * 512
                    bv, bb = ring_load(wv[:, :, cb0:cb0 + 512], C, 512)
                    for j in range(4):
                        ch = half * 4 + j
                        bk = pbank(0, 4)
                        for kc in range(C):
                            mm(ps[:, bk, :nt], bv[:, kc, j * 128:(j + 1) * 128], hT[:, kc, :nt],
                               kc == 0, kc == C - 1, (bb, B_hT), (PS[bk],))
                        ov, obuf = stb()
                        sc = 0.125 if which == 0 else 1.0
                        ACT(lambda e, ov=ov, bk=bk, sc=sc: e.activation(
                            ov[:, :nt], ps[:, bk, :nt], AF.Copy, scale=sc), (PS[bk],), (obuf,))
                        SPD(dst_s[ch, :, g0:g0 + nt], ov[:, :nt], (obuf,), (dst_b,))
                    if which == 1 and is_p:
                        for sub in range(nt // 128):
                            bk = pbank(4, 4)
                            for kc in range(C):
                                mm(ps[:, bk, :], hT[:, kc, sub * 128:(sub + 1) * 128], bv[:, kc, :],
                                   kc == 0, kc == C - 1, (bb, B_hT), (PS[bk],))
                            fv, fbuf = stf()
                            DVE(lambda e, fv=fv, bk=bk: e.tensor_copy(fv, ps[:, bk, :]), (PS[bk],), (fbuf,))
                            b_i, t0 = sub // 2, (sub % 2) * 128
                            if st.get("nk", True):
                                SPD(nk[b_i, l, t0:t0 + 128, half * 512:(half + 1) * 512], fv, (fbuf,), ())
            for half in range(2):
                cb0 = 4096 + half * 512
                bv, bb = ring_load(wv[:, :, cb0:cb0 + 512], C, 512)
                for sub in range(nt // 128):
                    bk = pbank(4, 4)
                    for kc in range(C):
                        mm(ps[:, bk, :], hT[:, kc, sub * 128:(sub + 1) * 128], bv[:, kc, :],
                           kc == 0, kc == C - 1, (bb, B_hT), (PS[bk],))
                    ov, obuf = stb()
                    ACT(lambda e, ov=ov, bk=bk: e.activation(ov, ps[:, bk, :], AF.Copy), (PS[bk],), (obuf,))
                    SPD(vt_s[l][g0 + sub * 128:g0 + (sub + 1) * 128, half * 512:(half + 1) * 512], ov,
                        (obuf,), (B_vt[l][tile],))
                    if is_p:
                        fv, fbuf = stf()
                        DVE(lambda e, fv=fv, bk=bk: e.tensor_copy(fv, ps[:, bk, :]), (PS[bk],), (fbuf,))
                        b_i, t0 = sub // 2, (sub % 2) * 128
                        if st.get("nv", True):
                            SPD(nv[b_i, l, t0:t0 + 128, half * 512:(half + 1) * 512], fv, (fbuf,), ())

        def mixer_core(nt, l, tile):
            is_p = tile == 0
            g0 = tile * 512
            switch_big(B_convh + B_naoT + CONV_BUFS + ATT_BUFS)
            hsrc = hcT_s[l].rearrange("c p t -> p c t")
            hb3 = None
            if is_p:
                hb3 = hcbuf[:, :, 0:572].rearrange("p a (b t) -> p a b t", t=286)
                DVE(lambda e: e.memset(hcbuf[:, :, 0:572], 0.0), (), (B_hcbuf,))
                for b_i in range(2):
                    SPD(hb3[:, :, b_i, 15:271], hsrc[:, :, g0 + b_i * 256:g0 + (b_i + 1) * 256],
                        (B_hcT[l][tile],), (B_hcbuf,))
            else:
                lt0 = (tile - 1) * 512
                if lt0 == 0:
                    DVE(lambda e: e.memset(hcbuf[:, :, 0:15], 0.0), (), (B_hcbuf,))
                    SPD(hcbuf[:, :, 15:15 + nt + 15], hsrc[:, :, g0:g0 + nt + 15],
                        (B_hcT[l][tile], B_hcT[l][tile + 1]), (B_hcbuf,))
                else:
                    rd = [B_hcT[l][tile - 1], B_hcT[l][tile]]
                    if tile + 1 < NTOK // 512:
                        rd.append(B_hcT[l][tile + 1])
                    SPD(hcbuf[:, :, 0:nt + 30], hsrc[:, :, g0 - 15:g0 + nt + 15], tuple(rd), (B_hcbuf,))
            wset = 0 if is_p else 1
            cbv = convv[:, 0, l, :]
            lng = convv[:, 1, l, :]
            lnb = convv[:, 2, l, :]
            conv_ops = []
            for chp in range(4):
                for jt in range(31):
                    for q in range(2):
                        ch = chp * 2 + q
                        wj = convw[:, wset, l, ch, jt:jt + 1]
                        if is_p:
                            acc = convacc[:, ch, :].rearrange("p (b t) -> p b t", t=256)
                            src = hb3[:, ch, :, jt:jt + 256]
                        else:
                            acc = convacc[:, ch, :nt]
                            src = hcbuf[:, ch, jt:jt + nt]
                        if jt == 0:
                            conv_ops.append(lambda acc=acc, src=src, wj=wj, ch=ch: DVE(
                                lambda e: e.tensor_scalar(acc, src, wj, cbv[:, ch:ch + 1], ALU.mult, ALU.add),
                                (B_hcbuf, B_const), (B_convacc[ch],)))
                        else:
                            conv_ops.append(lambda acc=acc, src=src, wj=wj, ch=ch: DVE(
                                lambda e: e.scalar_tensor_tensor(acc, src, wj, acc, ALU.mult, ALU.add),
                                (B_hcbuf, B_const, B_convacc[ch]), (B_convacc[ch],)))
            conv_it = iter(conv_ops)

            def conv_some(k):
                for _ in range(k):
                    f = next(conv_it, None)
                    if f is None:
                        return
                    f()

            qsrc = qT_s[l]
            ksrc = kT_s[l]
            npair = nt // 128
            m0 = 0
            cm_lo = 0
            if is_p:
                kt0, nkc = g0, 4
                kbufs = [B_kT[l][0]]
                vbufs = [B_vt[l][0]]
            else:
                m0 = (tile - 1) * 4
                cm_lo = max(m0 - 2, 0)
                cm_hi = max(m0 + npair - 1 - 2, 0) + 4
                nkc = cm_hi - cm_lo + 1
                kt0 = 512 + cm_lo * 128
                tl = sorted(set((kt0 + i * 128) // 512 for i in range(nkc)))
                kbufs = [B_kT[l][t] for t in tl]
                vbufs = [B_vt[l][t] for t in tl]

            def loads(j):
                bj = B_kvq[j % 2]
                SPD(qb[:, j % 2, :nt], qsrc[j, :, g0:g0 + nt], (B_qT[l][tile],), (bj,))
                SPD(kb[:, j % 2, :nkc * 128], ksrc[j, :, kt0:kt0 + nkc * 128], tuple(kbufs), (bj,))
                SPD(vb[:, j % 2, :nkc, :],
                    vt_s[l][kt0:kt0 + nkc * 128, j * 128:(j + 1) * 128].rearrange("(c p) d -> p c d", p=128),
                    tuple(vbufs), (bj,))
                if not is_p:
                    SPD(ckb[:, j % 2, :], ckT_s[l][j], (B_ckT[l],), (bj,))
                    SPD(cvbuf[:, j % 2, :, :], cvb_s[l][j], (B_cvb[l],), (bj,))

            units = []
            for j in range(8):
                for hh in range(2):
                    if is_p:
                        for b_i in range(2):
                            for kk in range(2):
                                units.append(dict(kind="p", j=j, hh=hh, b_i=b_i, kk=kk, first=(b_i == 0 and kk == 0),
                                                  last=(b_i == 1 and kk == 1)))
                    else:
                        for cc in range(4):
                            units.append(dict(kind="c", j=j, hh=hh, cc=cc, first=(cc == 0), last=False))
                        for pi in range(npair):
                            units.append(dict(kind="l", j=j, hh=hh, pi=pi, first=False, last=(pi == npair - 1)))
            nu = len(units)
            per_unit = -(-len(conv_ops) // nu)
            state = dict(last_banks=set(), x1=0, x2=0, rot=0)

            def pick_banks(kind):
                if kind == "l":
                    cands = [(0, 1), (2, 3)]
                    if state["rot"] % 2:
                        cands = cands[::-1]
                else:
                    r = state["rot"] % 4
                    cands = [((r + i) % 4,) for i in range(4)]
                state["rot"] += 1
                for c_ in cands:
                    if not (set(c_) & state["last_banks"]):
                        state["last_banks"] = set(c_)
                        return c_
                state["last_banks"] = set(cands[0])
                return cands[0]

            def emit_S(u):
                j, hh = u["j"], u["hh"]
                bj = B_kvq[j % 2]
                jb = j % 2
                pr = slice(hh * 64, hh * 64 + 64)
                u["pr"] = pr
                if u["kind"] == "p":
                    (bk,) = pick_banks("p")
                    u["bk"] = bk
                    qs = slice(u["b_i"] * 256, (u["b_i"] + 1) * 256)
                    u["qs"] = qs
                    kc = u["b_i"] * 2 + u["kk"]
                    u["kc"] = kc
                    mm(ps[:, bk, 0:256], kb[pr, jb, kc * 128:(kc + 1) * 128], qb[pr, jb, qs],
                       True, True, (bj,), (PS[bk],))
                    x = state["x1"] % 2
                    state["x1"] += 1
                    u["x"] = x
                    ACT(lambda e, x=x, bk=bk: e.activation(pc[:, x, 0:256], ps[:, bk, 0:256], AF.Exp),
                        (PS[bk],), (B_pc[x],))
                elif u["kind"] == "c":
                    (bk,) = pick_banks("c")
                    cc = u["cc"]
                    mm(ps[:, bk, :nt], ckb[pr, jb, cc * 128:(cc + 1) * 128], qb[pr, jb, :nt],
                       True, True, (bj,), (PS[bk],))
                    x = state["x1"] % 2
                    state["x1"] += 1
                    u["x"] = x
                    ACT(lambda e, x=x, bk=bk: e.activation(pc[:, x, :nt], ps[:, bk, :nt], AF.Exp),
                        (PS[bk],), (B_pc[x],))
                else:
                    pb0, pb1 = pick_banks("l")
                    pi = u["pi"]
                    h = 2 * j + hh
                    m = m0 + pi
                    cm0 = max(m - 2, 0)
                    u["cm0"] = cm0
                    x = state["x2"] % 2
                    state["x2"] += 1
                    u["x"] = x
                    bsrc = bias_first[l, h, m] if m < 4 else bias_gen[l, h]
                    SPD(biasb[:, x, :], bsrc, (), (B_biasb[x],))
                    sreg = ps_flat[:, pb0 * 512: pb0 * 512 + 640]
                    qs = slice(pi * 128, (pi + 1) * 128)
                    u["qs"] = qs
                    for ci in range(5):
                        kc = cm0 + ci - cm_lo
                        bank_w = PS[pb0] if ci < 4 else PS[pb1]
                        mm(sreg[:, ci * 128:(ci + 1) * 128], kb[pr, jb, kc * 128:(kc + 1) * 128],
                           qb[pr, jb, qs], True, True, (bj,), (bank_w,))
                    DVE(lambda e, x=x, sreg=sreg: e.tensor_tensor(
                        sbb[:, x, 0:512], sreg[:, 0:512], biasb[:, x, 0:512], ALU.add),
                        (PS[pb0], B_biasb[x]), (B_sbb[x],))
                    DVE(lambda e, x=x, sreg=sreg: e.tensor_tensor(
                        sbb[:, x, 512:640], sreg[:, 512:640], biasb[:, x, 512:640], ALU.add),
                        (PS[pb1], B_biasb[x]), (B_sbb[x],))
                    ACT(lambda e, x=x: e.activation(pb[:, x, :], sbb[:, x, :], AF.Exp),
                        (B_sbb[x],), (B_pb[x],))
                conv_some(per_unit)

            def emit_PV(u):
                j, hh = u["j"], u["hh"]
                bj = B_kvq[j % 2]
                jb = j % 2
                bO = 4 + hh * 2
                bD = bO + 1
                x = u["x"]
                if u["kind"] == "p":
                    qs, kc, kk = u["qs"], u["kc"], u["kk"]
                    mm(ps[:, bO, qs], vb[:, jb, kc, :], pc[:, x, 0:256], kk == 0, kk == 1,
                       (bj, B_pc[x]), (PS[bO],))
                    mm(ps[:, bD, qs], ones_bf[:], pc[:, x, 0:256], kk == 0, kk == 1,
                       (B_pc[x], B_const), (PS[bD],))
                elif u["kind"] == "c":
                    cc = u["cc"]
                    mm(ps[:, bO, :nt], cvbuf[:, jb, cc, :], pc[:, x, :nt], cc == 0, False,
                       (bj, B_pc[x]), (PS[bO],))
                    mm(ps[:, bD, :nt], ones_bf[:], pc[:, x, :nt], cc == 0, False,
                       (B_pc[x], B_const), (PS[bD],))
                else:
                    qs, cm0 = u["qs"], u["cm0"]
                    for ci in range(5):
                        kc = cm0 + ci - cm_lo
                        lastc = ci == 4 and u["last"]
                        mm(ps[:, bO, qs], vb[:, jb, kc, :], pb[:, x, ci * 128:(ci + 1) * 128], False, lastc,
                           (bj, B_pb[x]), (PS[bO],))
                        mm(ps[:, bD, qs], ones_bf[:], pb[:, x, ci * 128:(ci + 1) * 128], False, lastc,
                           (B_pb[x], B_const), (PS[bD],))
                if u["last"]:
                    pr = u["pr"]
                    DVE(lambda e, pr=pr, bD=bD: e.reciprocal(rcp[pr, :nt], ps[pr, bD, :nt]), (PS[bD],), (B_rcp,))
                    DVE(lambda e, pr=pr, bO=bO, j=j: e.tensor_tensor(
                        naoT[pr, j, :nt], ps[pr, bO, :nt], rcp[pr, :nt], ALU.mult),
                        (PS[bO], B_rcp), (B_naoT[j],))
                    if hh == 1 and j + 2 < 8:
                        loads(j + 2)

            loads(0)
            loads(1)
            emit_S(units[0])
            for i, u in enumerate(units):
                if i + 1 < nu:
                    emit_S(units[i + 1])
                emit_PV(u)
            conv_some(len(conv_ops))

            for ch in range(8):
                ACT(lambda e, ch=ch: e.activation(sq[:, 0, :nt], convacc[:, ch, :nt], AF.Copy),
                    (B_convacc[ch],), (B_sq[0],))
                DVE(lambda e, ch=ch: e.tensor_tensor(
                    sq[:, 1, :nt], convacc[:, ch, :nt], convacc[:, ch, :nt], ALU.mult),
                    (B_convacc[ch],), (B_sq[1],))
                mm(ps[:, 0, :nt], ones_bf[:], sq[:, 0, :nt], ch == 0, ch == 7, (B_sq[0], B_const), (PS[0],))
                mm(ps[:, 1, :nt], ones_bf[:], sq[:, 1, :nt], ch == 0, ch == 7, (B_sq[1], B_const), (PS[1],))
            ACT(lambda e: e.activation(meant[:, :nt], ps[:, 0, :nt], AF.Copy, scale=1.0 / 1024),
                (PS[0],), (B_mean,))
            DVE(lambda e: e.tensor_tensor(tmpf[:, 0, :nt], meant[:, :nt], meant[:, :nt], ALU.mult),
                (B_mean,), (B_tmpf[0],))
            DVE(lambda e: e.scalar_tensor_tensor(tmpf[:, 1, :nt], ps[:, 1, :nt], 1.0 / 1024, tmpf[:, 0, :nt],
                                                 ALU.mult, ALU.subtract), (PS[1], B_tmpf[0]), (B_tmpf[1],))
            ACT(lambda e: e.activation(stdt[:, :nt], tmpf[:, 1, :nt], AF.Sqrt, bias=EPS), (B_tmpf[1],), (B_std,))
            DVE(lambda e: e.reciprocal(rstd[:, :nt], stdt[:, :nt]), (B_std,), (B_rstd,))
            for ch in range(8):
                DVE(lambda e, ch=ch: e.tensor_tensor(convacc[:, ch, :nt], convacc[:, ch, :nt], meant[:, :nt],
                                                     ALU.subtract), (B_convacc[ch], B_mean), (B_convacc[ch],))
                DVE(lambda e, ch=ch: e.tensor_tensor(convacc[:, ch, :nt], convacc[:, ch, :nt], rstd[:, :nt],
                                                     ALU.mult), (B_convacc[ch], B_rstd), (B_convacc[ch],))
                ACT(lambda e, ch=ch: e.activation(convh[:, ch, :nt], convacc[:, ch, :nt], AF.Silu,
                                                  bias=lnb[:, ch:ch + 1], scale=lng[:, ch:ch + 1]),
                    (B_convacc[ch], B_const), (B_convh[ch],))

        def merge_out(nt, l, s):
            switch_big(B_convh + B_naoT + MRG_BUFS)
            wv = w_in[l].rearrange("(c p) n -> p c n", p=128)
            cov = w_co[l].rearrange("(c p) n -> p c n", p=128)
            nov = w_no[l].rearrange("(c p) n -> p c n", p=128)
            wov = w_o[l].rearrange("(c p) n -> p c n", p=128)
            G = dv[:, l, s, 5, :]
            n = [0]
            for g4 in range(4):
                for br in range(2):
                    gcol = (5120 if br == 0 else 7168) + g4 * 512
                    gv, gb = ring_load(wv[:, :, gcol:gcol + 512], C, 512)
                    pv, pbuf = ring_load((cov if br == 0 else nov)[:, :, g4 * 512:(g4 + 1) * 512], 8, 512)
                    act_in = convh if br == 0 else naoT
                    act_b = B_convh if br == 0 else B_naoT
                    for jj in range(4):
                        c = g4 * 4 + jj
                        ba = (n[0] % 2) * 2
                        bb_ = ba + 1
                        n[0] += 1
                        for kc in range(C):
                            mm(ps[:, ba, :nt], gv[:, kc, jj * 128:(jj + 1) * 128], hT[:, kc, :nt],
                               kc == 0, kc == C - 1, (gb, B_hT), (PS[ba],))
                        for kc in range(8):
                            mm(ps[:, bb_, :nt], pv[:, kc, jj * 128:(jj + 1) * 128], act_in[:, kc, :nt],
                               kc == 0, kc == 7, (pbuf, act_b[kc]), (PS[bb_],))
                        x = n[0] % 2
                        ACT(lambda e, x=x, ba=ba: e.activation(tmpf[:, x, :nt], ps[:, ba, :nt], AF.Sigmoid),
                            (PS[ba],), (B_tmpf[x],))
                        if br == 0:
                            DVE(lambda e, x=x, bb_=bb_, jj=jj: e.tensor_tensor(
                                t1[:, jj, :nt], tmpf[:, x, :nt], ps[:, bb_, :nt], ALU.mult),
                                (B_tmpf[x], PS[bb_]), (B_t1[jj],))
                        else:
                            DVE(lambda e, x=x, bb_=bb_: e.tensor_tensor(
                                tmpf[:, x, :nt], tmpf[:, x, :nt], ps[:, bb_, :nt], ALU.mult),
                                (B_tmpf[x], PS[bb_]), (B_tmpf[x],))
                            DVE(lambda e, x=x, jj=jj, c=c: e.tensor_tensor(
                                mT[:, c, :nt], tmpf[:, x, :nt], t1[:, jj, :nt], ALU.add),
                                (B_tmpf[x], B_t1[jj]), (B_mT[c],))
            for g4 in range(4):
                ov, ob = ring_load(wov[:, :, g4 * 512:(g4 + 1) * 512], C, 512)
                for jj in range(4):
                    c = g4 * 4 + jj
                    bk = 4 + c % 4
                    for kc in range(C):
                        mm(ps[:, bk, :nt], ov[:, kc, jj * 128:(jj + 1) * 128], mT[:, kc, :nt],
                           kc == 0, kc == C - 1, (ob, B_mT[kc]), (PS[bk],))
                    DVE(lambda e, c=c, bk=bk: e.scalar_tensor_tensor(
                        xT[:, c, :nt], ps[:, bk, :nt], G[:, c:c + 1], xT[:, c, :nt], ALU.mult, ALU.add),
                        (PS[bk], B_xT[c], B_dv), (B_xT[c],))

        def final_out(nt, tile):
            rms_stats(nt)
            switch_big(B_ostage)
            for c in range(C):
                DVE(lambda e, c=c: e.scalar_tensor_tensor(
                    xT[:, c, :nt], xT[:, c, :nt], final_g[:, c:c + 1], rstd[:, :nt], ALU.mult, ALU.mult),
                    (B_xT[c], B_rstd, B_const), (B_xT[c],))
            dst = yp if tile == 0 else ys
            r0 = 0 if tile == 0 else (tile - 1) * 512
            n = 0
            for sub in range(nt // 128):
                for c4 in range(4):
                    bank = n % 4
                    n += 1
                    for q in range(4):
                        c = c4 * 4 + q
                        PE(lambda e, bank=bank, q=q, c=c, sub=sub: e.transpose(
                            ps[:, bank, q * 128:(q + 1) * 128], xT[:, c, sub * 128:(sub + 1) * 128], ident[:]),
                            (B_xT[c], B_const), (PS[bank],))
                    o_ap = ostage[:, sub % 2, c4 * 512:(c4 + 1) * 512]
                    if c4 % 2 == 0:
                        ACT(lambda e, o_ap=o_ap, bank=bank: e.activation(o_ap, ps[:, bank, :], AF.Copy),
                            (PS[bank],), (B_ostage[sub % 2],))
                    else:
                        DVE(lambda e, o_ap=o_ap, bank=bank: e.tensor_copy(o_ap, ps[:, bank, :]),
                            (PS[bank],), (B_ostage[sub % 2],))
                SPD(dst[r0 + sub * 128:r0 + (sub + 1) * 128, :], ostage[:, sub % 2, :], (B_ostage[sub % 2],), ())

        def sset(tile):
            return 0 if tile == 0 else 1

        st = stages or {}
        if dbg:
            dbg_dv = dout("dbg_dv", [128, DEPTH * 2 * 9 * C])
            SPD(dbg_dv, dv[:].rearrange("p a b c d -> p (a b c d)"), (B_dv,), ())
        parts = st.get("parts", "xfsi")
        for tile in st.get("p0", range(6)):
            if "x" in parts:
                load_x_input(tile)
            if "f" in parts:
                ffn(512, 0, 0, sset(tile))
            if "s" in parts:
                store_x1(0, tile, 512)
            if "i" in parts:
                in_proj1(512, 0, sset(tile), tile)
        for tile in st.get("p1", range(6)):
            nt = 256 if tile == 5 else 512
            load_x1(0, tile, nt)
            mixer_core(nt, 0, tile)
            merge_out(nt, 0, sset(tile))
            ffn(nt, 0, 1, sset(tile))
            ffn(nt, 1, 0, sset(tile))
            store_x1(1, tile, nt)
            in_proj1(nt, 1, sset(tile), tile)
        for tile in st.get("p2", range(5)):
            load_x1(1, tile, 512)
            mixer_core(512, 1, tile)
            merge_out(512, 1, sset(tile))
            ffn(512, 1, 1, sset(tile))
            final_out(512, tile)

        emit_program(nc, P, esem, dsem, block)
    nc._k_declared = list(declared)
    return nc


def _fm(v):
    v = np.asarray(v, np.float32)
    lead = v.shape[:-1]
    n = v.shape[-1] // 128
    v = v.reshape(lead + (n, 128))
    return np.ascontiguousarray(np.moveaxis(v, -1, 0))


def _bias_tables(rel_bias, half):
    def table(m):
        cm0 = max(m - 2, 0)
        key = np.arange(128)
        qry = np.arange(128)
        ch = np.arange(5)
        rk_l = 2 * (cm0 + ch)[None, :, None] + (key // 64)[:, None, None]
        ck_l = (key % 64)[:, None, None] + 0 * ch[None, :, None]
        rq_l = (2 * m + qry // 64)[None, None, :]
        cq_l = (qry % 64)[None, None, :]
        if half == 0:
            rk, ckk, rq, cq = rk_l, ck_l, rq_l, cq_l
        else:
            rk, ckk, rq, cq = 63 - rk_l, 63 - ck_l, 63 - rq_l, 63 - cq_l
        rs = np.clip(rq - 4, 0, 56)
        cs = np.clip(cq - 8, 0, 48)
        valid = (rk >= rs) & (rk <= rs + 7) & (ckk >= cs) & (ckk <= cs + 15)
        dr = np.clip(rk - rq + 7, 0, 14)
        dc = np.clip(ckk - cq + 15, 0, 30)
        dr, dc, valid = np.broadcast_arrays(dr, dc, valid)
        g = rel_bias[:, :, dr, dc]
        g = np.where(valid[None, None], g, np.float32(NEG)).astype(np.float32)
        return g.reshape(DEPTH, NH, 128, 640)
    first = np.stack([table(m) for m in range(4)], axis=2)
    gen = table(8)
    return np.ascontiguousarray(first), np.ascontiguousarray(gen)


_NC_CACHE = {}


def _get_nc(dbg=False):
    if dbg not in _NC_CACHE:
        _NC_CACHE[dbg] = build_program(dbg)
    return _NC_CACHE[dbg]


def make_in_maps(x_prompt, x_sample, cache_k, cache_v, c, c_ctx, w_ada, b_ada, norm_g,
                 ffn_w_gate, ffn_w_up, ffn_w_down, w_in, conv_w, conv_b, conv_ln_g,
                 conv_ln_b, w_conv_out, rel_bias, w_na_out, w_out, final_g):
    f = lambda a: np.ascontiguousarray(np.asarray(a, np.float32))
    x_prompt, x_sample, cache_k, cache_v = f(x_prompt), f(x_sample), f(cache_k), f(cache_v)
    c, c_ctx, rel_bias, conv_w = f(c), f(c_ctx), f(rel_bias), f(conv_w)
    shared = {
        "w_ada": f(w_ada), "wg": f(ffn_w_gate), "wu": f(ffn_w_up), "wd": f(ffn_w_down),
        "w_in": f(w_in), "w_co": f(w_conv_out), "w_no": f(w_na_out), "w_o": f(w_out),
        "b_ada_t": _fm(b_ada), "norm_g_t": _fm(norm_g), "final_g_t": _fm(final_g),
        "convv_t": _fm(np.stack([f(conv_b), f(conv_ln_g), f(conv_ln_b)], 0)),
        "ident": np.eye(128, dtype=np.float32),
    }
    def taps(cw):
        t = np.transpose(cw, (0, 2, 1)).reshape(DEPTH, 8, 128, 31)
        return np.ascontiguousarray(np.transpose(t, (2, 0, 1, 3)))
    tp_nat = taps(conv_w)
    tp_rev = taps(conv_w[:, ::-1, :])
    tables = [_bias_tables(rel_bias, 0), _bias_tables(rel_bias, 1)]
    in_maps = []
    for i in range(8):
        b, half = i // 2, i % 2
        if half == 0:
            xs = x_sample[b, 0:2560]
            cws = tp_nat
        else:
            xs = x_sample[b, ::-1][0:2560]
            cws = tp_rev
        m = dict(shared)
        m["xp"] = np.ascontiguousarray(x_prompt[2 * i:2 * i + 2].reshape(512, D))
        m["xs"] = np.ascontiguousarray(xs)
        m["ck"] = np.ascontiguousarray(cache_k[b].reshape(DEPTH, 512, 1024))
        m["cv"] = np.ascontiguousarray(cache_v[b].reshape(DEPTH, 512, 1024))
        m["cvec"] = np.ascontiguousarray(np.stack([_fm(c_ctx), _fm(c[b])], axis=-1))
        m["convw_t"] = np.ascontiguousarray(np.stack([tp_nat, cws], axis=1))
        m["bias_first"], m["bias_gen"] = tables[half]
        in_maps.append(m)
    return in_maps


def assemble(results):
    y_p = np.zeros((16, 256, D), np.float32)
    y_s = np.zeros((4, 4096, D), np.float32)
    n_k = np.zeros((16, DEPTH, 256, NH, 64), np.float32)
    n_v = np.zeros((16, DEPTH, 256, NH, 64), np.float32)
    for i, r in enumerate(results):
        b, half = i // 2, i % 2
        y_p[2 * i:2 * i + 2] = np.asarray(r["yp"]).reshape(2, 256, D)
        ysl = np.asarray(r["ys"])
        if half == 0:
            y_s[b, 0:2048] = ysl
        else:
            y_s[b, 2048:4096] = ysl[::-1]
        n_k[2 * i:2 * i + 2] = np.asarray(r["nk"]).reshape(2, DEPTH, 256, NH, 64)
        n_v[2 * i:2 * i + 2] = np.asarray(r["nv"]).reshape(2, DEPTH, 256, NH, 64)
    return y_p, y_s, n_k, n_v


def kernel(**inputs):
    nc = _get_nc(False)
    in_maps = make_in_maps(**inputs)
    names = set(nc._k_declared)
    in_maps = [{k: v for k, v in m.items() if k in names} for m in in_maps]
    res = run_bass_kernel_spmd(nc, in_maps, core_ids=list(range(8)))
    return assemble(res.results)
```

```python
import contextlib
import numpy as np
import concourse.bass as bass
import concourse.mybir as mybir
from concourse.bass_utils import run_bass_kernel_spmd

F32 = mybir.dt.float32
BF16 = mybir.dt.bfloat16
AF = mybir.ActivationFunctionType
ALU = mybir.AluOpType

D = 2048
C = 16
DFF = 5504
NF = 43
DEPTH = 2
NH = 16
EPS = 1e-6
NEG = -30000.0
NTOK = 3072
RING_SLOTS = 4
RING_W = 8192

ENGS = ("pe", "act", "dve", "pool", "sp")
DMA_K = {"pool": 8, "sp": 16}


class Buf:
    __slots__ = ("name", "w", "r", "rd", "excl")

    def __init__(self, name, excl=False):
        self.name = name
        self.excl = excl
        self.w = None
        self.r = {}
        self.rd = []


class Ins:
    __slots__ = ("eng", "fn", "deps", "flag", "val", "dma", "qi", "seq")


class Prog:
    def __init__(self):
        self.q = {e: [] for e in ENGS}
        self.ndma = {e: 0 for e in ENGS}
        self.nseq = 0
        self.region_latest = {}

    def add(self, eng, fn, reads=(), writes=(), dma=False):
        ins = Ins()
        ins.eng = eng
        ins.fn = fn
        ins.dma = dma
        ins.flag = dma
        ins.val = None
        ins.qi = None
        ins.seq = self.nseq
        self.nseq += 1
        if dma:
            ins.qi = self.ndma[eng]
            self.ndma[eng] += 1
        deps = []
        for b in reads:
            if b.w is not None:
                deps.append(b.w)
            if b.excl:
                deps.extend(r for e2, r in b.r.items() if e2 != eng)
        for b in writes:
            if b.w is not None:
                deps.append(b.w)
            deps.extend(b.r.values())
            deps.extend(b.rd)
        dd = []
        seen = set()
        for d in deps:
            if d is ins or id(d) in seen:
                continue
            seen.add(id(d))
            if (not d.dma) and (not dma) and d.eng == "pe" and eng == "pe":
                continue
            d.flag = True
            dd.append(d)
        ins.deps = dd
        for b in reads:
            if dma:
                b.rd.append(ins)
            else:
                b.r[eng] = ins
        for b in writes:
            b.w = ins
            b.r = {}
            b.rd = []
        self.q[eng].append(ins)
        return ins

    def handoff(self, olds, news):
        keep = set(id(b) for b in olds) & set(id(b) for b in news)
        olds = [b for b in olds if id(b) not in keep]
        news = [b for b in news if id(b) not in keep]
        pend = []
        for b in olds:
            if b.w is not None:
                pend.append(b.w)
            pend.extend(b.r.values())
            pend.extend(b.rd)
            b.w = None
            b.r = {}
            b.rd = []
        latest = dict(self.region_latest)
        dmas = []
        for a in pend:
            if a.dma:
                dmas.append(a)
            elif a.eng not in latest or a.seq > latest[a.eng].seq:
                latest[a.eng] = a
        self.region_latest = latest
        pend = list(latest.values()) + dmas
        for b in news:
            b.w = None
            b.r = {}
            b.rd = list(pend)


def emit_program(nc, P, esem, dsem, block):
    for e in ENGS:
        cnt = 0
        for ins in P.q[e]:
            if ins.dma:
                continue
            if ins.flag:
                cnt += 1
                ins.val = cnt

    def event(d):
        if d.dma:
            k = DMA_K[d.eng]
            return dsem[d.eng][d.qi % k], 16 * (d.qi // k + 1)
        return esem[d.eng], d.val

    def run(eng_name, e):
        waited = {}

        def wait(sem, val):
            key = id(sem)
            if waited.get(key, 0) >= val:
                return
            waited[key] = val
            e.wait_ge(sem, val)

        for ins in P.q[eng_name]:
            for d in ins.deps:
                s, v = event(d)
                wait(s, v)
            if ins.dma:
                k = DMA_K[eng_name]
                if ins.qi >= k:
                    wait(dsem[eng_name][ins.qi % k], 16 * (ins.qi // k))
                bi = ins.fn(e)
                bi.then_inc(dsem[eng_name][ins.qi % k], 16)
            else:
                bi = ins.fn(e)
                if ins.flag:
                    bi.then_inc(esem[eng_name], 1)
        if eng_name in DMA_K:
            k = DMA_K[eng_name]
            n = P.ndma[eng_name]
            for j in range(min(k, n)):
                cntj = len(range(j, n, k))
                wait(dsem[eng_name][j], 16 * cntj)

    @block.tensor
    def _(e):
        run("pe", e)

    @block.scalar
    def _(e):
        run("act", e)

    @block.vector
    def _(e):
        run("dve", e)

    @block.gpsimd
    def _(e):
        run("pool", e)

    @block.sync
    def _(e):
        run("sp", e)


def build_program(dbg=False, stages=None):
    nc = bass.Bass("TRN2", target_bir_lowering=False)
    P = Prog()
    nc._k_declared = None

    declared = []

    class LazyIn:
        def __init__(self, name, shape, dt):
            self.name, self.shape, self.dt, self._ap = name, list(shape), dt, None

        def get(self):
            if self._ap is None:
                self._ap = nc.dram_tensor(self.name, self.shape, self.dt, kind="ExternalInput").ap()
                declared.append(self.name)
            return self._ap

        def __getitem__(self, k):
            return self.get()[k]

        def rearrange(self, *a, **k):
            return self.get().rearrange(*a, **k)

    def din(name, shape, dt=F32):
        return LazyIn(name, shape, dt)

    def dout(name, shape, dt=F32):
        return nc.dram_tensor(name, list(shape), dt, kind="ExternalOutput").ap()

    def dscr(name, shape, dt):
        return nc.dram_tensor(name, list(shape), dt,
                              kind=("ExternalOutput" if dbg else "Internal")).ap()

    xp = din("xp", [512, D])
    xs = din("xs", [2560, D])
    ck_d = din("ck", [DEPTH, 512, 1024])
    cv_d = din("cv", [DEPTH, 512, 1024])
    cvec_d = din("cvec", [128, C, 2])
    w_ada = din("w_ada", [DEPTH, D, 9 * D])
    b_ada_d = din("b_ada_t", [128, DEPTH, 144])
    norm_g_d = din("norm_g_t", [128, DEPTH, 3, C])
    final_g_d = din("final_g_t", [128, C])
    wg = din("wg", [DEPTH, 2, D, DFF])
    wu = din("wu", [DEPTH, 2, D, DFF])
    wd = din("wd", [DEPTH, 2, DFF, D])
    w_in = din("w_in", [DEPTH, D, 9216])
    convw_d = din("convw_t", [128, 2, DEPTH, 8, 31])
    convv_d = din("convv_t", [128, 3, DEPTH, 8])
    w_co = din("w_co", [DEPTH, 1024, D])
    w_no = din("w_no", [DEPTH, 1024, D])
    w_o = din("w_o", [DEPTH, D, D])
    bias_first = din("bias_first", [DEPTH, NH, 4, 128, 640])
    bias_gen = din("bias_gen", [DEPTH, NH, 128, 640])
    ident_d = din("ident", [128, 128])

    yp = dout("yp", [512, D])
    ys = dout("ys", [2048, D])
    nk = dout("nk", [2, DEPTH, 256, 1024])
    nv = dout("nv", [2, DEPTH, 256, 1024])

    x1T_s = [dscr(f"x1T{l}", [C, 128, NTOK], F32) for l in range(DEPTH)]
    uT_s = [dscr(f"uT{l}", [C, 128, NTOK], BF16) for l in range(DEPTH)]
    hcT_s = [dscr(f"hcT{l}", [8, 128, NTOK], BF16) for l in range(DEPTH)]
    qT_s = [dscr(f"qT{l}", [8, 128, NTOK], BF16) for l in range(DEPTH)]
    kT_s = [dscr(f"kT{l}", [8, 128, NTOK], BF16) for l in range(DEPTH)]
    vt_s = [dscr(f"vt{l}", [NTOK, 1024], BF16) for l in range(DEPTH)]
    ckT_s = [dscr(f"ckT{l}", [8, 128, 512], BF16) for l in range(DEPTH)]
    cvb_s = [dscr(f"cvb{l}", [8, 128, 4, 128], BF16) for l in range(DEPTH)]

    def tilebufs(name):
        return [[Buf(f"{name}{l}_{t}") for t in range(NTOK // 512)] for l in range(DEPTH)]

    B_x1T = tilebufs("x1T")
    B_uT = tilebufs("uT")
    B_hcT = tilebufs("hcT")
    B_qT = tilebufs("qT")
    B_kT = tilebufs("kT")
    B_vt = tilebufs("vt")
    B_ckT = [Buf(f"ckT{l}") for l in range(DEPTH)]
    B_cvb = [Buf(f"cvb{l}") for l in range(DEPTH)]

    es = contextlib.ExitStack()
    with es:
        def sb(name, shape, dt):
            return es.enter_context(nc.sbuf_tensor("sb_" + name, list(shape), dt))

        ring = sb("ring", [128, RING_SLOTS, RING_W], BF16)
        xT = sb("xT", [128, C, 512], F32)
        hT = sb("hT", [128, C, 512], BF16)
        big = sb("big", [128, 36864], BF16)
        ident = sb("ident", [128, 128], F32)
        ones_bf = sb("ones_bf", [128, 128], BF16)
        scv = sb("scv", [128, C, 2], BF16)
        cvec = sb("cvec", [128, C, 2], F32)
        b_ada = sb("b_ada", [128, DEPTH, 144], F32)
        norm_g = sb("norm_g", [128, DEPTH, 3, C], F32)
        final_g = sb("final_g", [128, C], F32)
        convw = sb("convw", [128, 2, DEPTH, 8, 31], F32)
        convv = sb("convv", [128, 3, DEPTH, 8], F32)
        modv = sb("modv", [128, DEPTH, 2, 144], F32)
        dv = sb("dv", [128, DEPTH, 2, 9, C], F32)
        sq = sb("sq", [128, 2, 512], BF16)
        stdt = sb("stdt", [128, 512], F32)
        rstd = sb("rstd", [128, 512], F32)
        meant = sb("meant", [128, 512], F32)
        tmpf = sb("tmpf", [128, 2, 512], F32)
        ps = es.enter_context(nc.psum_tensor("ps", [128, 8, 512], F32))
        ps_flat = ps[:].rearrange("p a b -> p (a b)")

        esem = {e: es.enter_context(nc.semaphore(f"e_{e}")) for e in ("pe", "act", "dve")}
        dsem = {q: [es.enter_context(nc.semaphore(f"d_{q}{i}")) for i in range(k)]
                for q, k in DMA_K.items()}
        block = es.enter_context(nc.Block())

        PS = [Buf(f"ps{i}", excl=True) for i in range(8)]
        B_ring = [Buf(f"ring{i}") for i in range(RING_SLOTS)]
        B_xT = [Buf(f"xT{c}") for c in range(C)]
        B_hT = Buf("hT")
        B_sq = [Buf("sq0"), Buf("sq1")]
        B_std = Buf("std")
        B_rstd = Buf("rstd")
        B_mean = Buf("mean")
        B_tmpf = [Buf("tmpf0"), Buf("tmpf1")]
        B_const = Buf("const")
        B_dv = Buf("dv")
        B_modv = Buf("modv")
        B_scv = Buf("scv")

        def PE(fn, r=(), w=()):
            return P.add("pe", fn, r, w)

        def ACT(fn, r=(), w=()):
            return P.add("act", fn, r, w)

        def DVE(fn, r=(), w=()):
            return P.add("dve", fn, r, w)

        def SPD(out, in_, r=(), w=()):
            if isinstance(in_, LazyIn):
                in_ = in_.get()
            return P.add("sp", lambda e, o=out, i=in_: e.dma_start(out=o, in_=i), r, w, dma=True)

        def mm(out, lhsT, rhs, start, stop, r, w):
            return PE(lambda e, o=out, a=lhsT, b=rhs, s=start, t=stop:
                      e.matmul(o, a, b, start=s, stop=t), r, w)

        ring_n = [0]

        def ring_load(src, k, n):
            i = ring_n[0] % RING_SLOTS
            ring_n[0] += 1
            view = ring[:, i, 0:k * n].rearrange("p (k n) -> p k n", n=n)
            P.add("pool", lambda e, o=view, s=src: e.dma_start(out=o, in_=s),
                  (), (B_ring[i],), dma=True)
            return view, B_ring[i]

        def bview(a, b):
            return big[:, a:b]

        aT = bview(0, 22016).rearrange("p (f t) -> p f t", t=512)
        sgt = bview(22016, 24064).bitcast(F32).rearrange("p (a t) -> p a t", t=512)
        xstage = bview(24064, 32256).bitcast(F32).rearrange("p (a t) -> p a t", t=2048)
        B_aT = [Buf(f"aT{f}") for f in range(NF)]
        B_sgt = [Buf("sgt0"), Buf("sgt1")]
        B_xstage = [Buf("xst0"), Buf("xst1")]
        FFN_BUFS = B_aT + B_sgt
        st_bf = bview(0, 4096).rearrange("p (a t) -> p a t", t=512)
        st_f32 = bview(4096, 8192).bitcast(F32).rearrange("p (a t) -> p a t", t=512)
        B_stbf = [Buf(f"stbf{i}") for i in range(8)]
        B_stf = [Buf(f"stf{i}") for i in range(4)]
        IP_BUFS = B_stbf + B_stf
        convh = bview(0, 4096).rearrange("p (a t) -> p a t", t=512)
        naoT = bview(4096, 8192).rearrange("p (a t) -> p a t", t=512)
        Y0 = 8192
        convacc = bview(24064, 32256).bitcast(F32).rearrange("p (a t) -> p a t", t=512)
        hcbuf = bview(32256, 36864).rearrange("p (a t) -> p a t", t=576)
        B_convh = [Buf(f"convh{i}") for i in range(8)]
        B_naoT = [Buf(f"naoT{i}") for i in range(8)]
        B_convacc = [Buf(f"convacc{i}") for i in range(8)]
        B_hcbuf = Buf("hcbuf")
        CONV_BUFS = B_convacc + [B_hcbuf]
        o = Y0
        kb = bview(o, o + 2048).rearrange("p (a t) -> p a t", t=1024); o += 2048
        vb = bview(o, o + 2048).rearrange("p (a c d) -> p a c d", c=8, d=128); o += 2048
        qb = bview(o, o + 1024).rearrange("p (a t) -> p a t", t=512); o += 1024
        ckb = bview(o, o + 1024).rearrange("p (a t) -> p a t", t=512); o += 1024
        cvbuf = bview(o, o + 1024).rearrange("p (a c d) -> p a c d", c=4, d=128); o += 1024
        biasb = bview(o, o + 2560).bitcast(F32).rearrange("p (a t) -> p a t", t=640); o += 2560
        sbb = bview(o, o + 2560).bitcast(F32).rearrange("p (a t) -> p a t", t=640); o += 2560
        pb = bview(o, o + 1280).rearrange("p (a t) -> p a t", t=640); o += 1280
        pc = bview(o, o + 1024).rearrange("p (a t) -> p a t", t=512); o += 1024
        rcp = bview(o, o + 1024).bitcast(F32); o += 1024
        assert o <= 24064
        B_kvq = [Buf("kvq0"), Buf("kvq1")]
        B_biasb = [Buf("biasb0"), Buf("biasb1")]
        B_sbb = [Buf("sbb0"), Buf("sbb1")]
        B_pb = [Buf("pb0"), Buf("pb1")]
        B_pc = [Buf("pc0"), Buf("pc1")]
        B_rcp = Buf("rcp")
        ATT_BUFS = B_kvq + B_biasb + B_sbb + B_pb + B_pc + [B_rcp]
        t1 = bview(Y0, Y0 + 4096).bitcast(F32).rearrange("p (a t) -> p a t", t=512)
        mT = bview(Y0 + 4096, Y0 + 4096 + 8192).rearrange("p (a t) -> p a t", t=512)
        B_t1 = [Buf(f"t1_{i}") for i in range(4)]
        B_mT = [Buf(f"mT{i}") for i in range(C)]
        MRG_BUFS = B_t1 + B_mT
        ostage = bview(0, 8192).bitcast(F32).rearrange("p (a t) -> p a t", t=2048)
        B_ostage = [Buf("ost0"), Buf("ost1")]
        cstage = bview(0, 8192).bitcast(F32).rearrange("p (c t) -> p c t", t=1024)
        cstb = bview(8192, 12288).rearrange("p (c t) -> p c t", t=1024)
        ckTt = bview(12288, 16384).rearrange("p (c t) -> p c t", t=512)
        B_cstage = Buf("cstage")
        B_cstb = Buf("cstb")
        B_ckTt = Buf("ckTt")

        cur_big = [[]]

        def switch_big(news):
            P.handoff(cur_big[0], news)
            cur_big[0] = list(news)

        cur_B = [[]]

        def switch_B(news):
            P.handoff(cur_B[0], news)
            cur_B[0] = list(news)

        bg = {"it": None, "k": 0}

        def bg_some(k=None):
            if bg["it"] is None:
                return
            for _ in range(bg["k"] if k is None else k):
                f = next(bg["it"], None)
                if f is None:
                    bg["it"] = None
                    return
                f()

        SPD(ident[:], ident_d, (), (B_const,))
        SPD(cvec[:], cvec_d, (), (B_const,))
        SPD(b_ada[:], b_ada_d, (), (B_const,))
        SPD(norm_g[:], norm_g_d, (), (B_const,))
        SPD(final_g[:], final_g_d, (), (B_const,))
        SPD(convw[:], convw_d, (), (B_const,))
        SPD(convv[:], convv_d, (), (B_const,))
        DVE(lambda e: e.memset(ones_bf[:], 1.0), (), (B_const,))
        ACT(lambda e: e.activation(scv[:], cvec[:], AF.Silu), (B_const,), (B_scv,))

        st = stages or {}
        psm = ps[:, 0, 0:288]
        for l in (range(DEPTH) if st.get("ada", True) else []):
            wv = w_ada[l].rearrange("(c p) n -> p c n", p=128)
            for blk in range(36):
                view, rb = ring_load(wv[:, :, blk * 512:(blk + 1) * 512], C, 512)
                for j in range(4):
                    col = (blk * 4 + j) * 2
                    for kc in range(C):
                        mm(psm[:, col:col + 2], view[:, kc, j * 128:(j + 1) * 128], scv[:, kc, :],
                           kc == 0, kc == C - 1, (rb, B_scv), (PS[0],))
            psm3 = psm.rearrange("p (j s) -> p j s", s=2)
            for s in range(2):
                DVE(lambda e, l=l, s=s, psm3=psm3: e.tensor_tensor(
                    modv[:, l, s, :], psm3[:, :, s], b_ada[:, l, :], ALU.add),
                    (PS[0], B_const), (B_modv,))
            for s in range(2):
                for k3 in range(3):
                    shift = modv[:, l, s, (3 * k3) * C:(3 * k3 + 1) * C]
                    scale = modv[:, l, s, (3 * k3 + 1) * C:(3 * k3 + 2) * C]
                    gate = modv[:, l, s, (3 * k3 + 2) * C:(3 * k3 + 3) * C]
                    DVE(lambda e, l=l, s=s, k3=k3, scale=scale: e.scalar_tensor_tensor(
                        dv[:, l, s, 3 * k3, :], scale, 1.0, norm_g[:, l, k3, :], ALU.add, ALU.mult),
                        (B_modv, B_const), (B_dv,))
                    DVE(lambda e, l=l, s=s, k3=k3, shift=shift: e.tensor_copy(
                        dv[:, l, s, 3 * k3 + 1, :], shift), (B_modv,), (B_dv,))
                    gsc = 1.0 if k3 == 1 else 0.5
                    DVE(lambda e, l=l, s=s, k3=k3, gate=gate, gsc=gsc: e.tensor_scalar(
                        dv[:, l, s, 3 * k3 + 2, :], gate, gsc, None, ALU.mult),
                        (B_modv,), (B_dv,))

        switch_big([B_cstage, B_cstb, B_ckTt])
        for l in (range(DEPTH) if st.get("cache", True) else []):
            SPD(cstage, ck_d[l].rearrange("(c p) n -> p c n", p=128), (), (B_cstage,))
            n = 0
            for fc in range(8):
                for tc in range(4):
                    bank = 4 + (n // 4) % 4
                    sl = n % 4
                    PE(lambda e, bank=bank, sl=sl, tc=tc, fc=fc: e.transpose(
                        ps[:, bank, sl * 128:(sl + 1) * 128], cstage[:, tc, fc * 128:(fc + 1) * 128],
                        ident[:]), (B_cstage, B_const), (PS[bank],))
                    n += 1
                    if sl == 3:
                        ACT(lambda e, bank=bank, fc=fc: e.activation(
                            ckTt[:, fc, :], ps[:, bank, :], AF.Copy), (PS[bank],), (B_ckTt,))
            SPD(ckT_s[l].rearrange("c p t -> p c t"), ckTt, (B_ckTt,), (B_ckT[l],))
            SPD(cstage, cv_d[l].rearrange("(c p) n -> p c n", p=128), (), (B_cstage,))
            DVE(lambda e: e.tensor_copy(cstb, cstage), (B_cstage,), (B_cstb,))
            SPD(cvb_s[l].rearrange("j p c d -> p c j d"),
                cstb.rearrange("p c (j d) -> p c j d", d=128), (B_cstb,), (B_cvb[l],))

        def rms_stats(nt):
            for c in range(C):
                DVE(lambda e, c=c: e.tensor_tensor(sq[:, c % 2, :nt], xT[:, c, :nt], xT[:, c, :nt],
                                                   ALU.mult), (B_xT[c],), (B_sq[c % 2],))
                mm(ps[:, 7, :nt], ones_bf[:], sq[:, c % 2, :nt], c == 0, c == C - 1,
                   (B_sq[c % 2], B_const), (PS[7],))
            ACT(lambda e: e.activation(stdt[:, :nt], ps[:, 7, :nt], AF.Sqrt, bias=EPS, scale=1.0 / D),
                (PS[7],), (B_std,))
            DVE(lambda e: e.reciprocal(rstd[:, :nt], stdt[:, :nt]), (B_std,), (B_rstd,))

        def modulate(nt, l, s, k3):
            A = dv[:, l, s, 3 * k3, :]
            Bv = dv[:, l, s, 3 * k3 + 1, :]
            for c in range(C):
                DVE(lambda e, c=c, A=A: e.scalar_tensor_tensor(
                    tmpf[:, c % 2, :nt], xT[:, c, :nt], A[:, c:c + 1], rstd[:, :nt], ALU.mult, ALU.mult),
                    (B_xT[c], B_rstd, B_dv), (B_tmpf[c % 2],))
                ACT(lambda e, c=c, Bv=Bv: e.activation(
                    hT[:, c, :nt], tmpf[:, c % 2, :nt], AF.Identity, bias=Bv[:, c:c + 1]),
                    (B_tmpf[c % 2], B_dv), (B_hT,))

        def ffn(nt, l, w, s):
            k3 = 0 if w == 0 else 2
            switch_big(FFN_BUFS)
            rms_stats(nt)
            modulate(nt, l, s, k3)
            wgv = wg[l, w].rearrange("(c p) n -> p c n", p=128)
            wuv = wu[l, w].rearrange("(c p) n -> p c n", p=128)
            wdv = wd[l, w].rearrange("(f p) n -> p f n", p=128)
            G = dv[:, l, s, 3 * k3 + 2, :]
            for blk in range(11):
                c0 = blk * 512
                wb = min(512, DFF - c0)
                gv, gb = ring_load(wgv[:, :, c0:c0 + wb], C, wb)
                uv, ub = ring_load(wuv[:, :, c0:c0 + wb], C, wb)
                for j in range(wb // 128):
                    f = blk * 4 + j
                    bg = (f % 2) * 2
                    bu = bg + 1
                    for kc in range(C):
                        mm(ps[:, bg, :nt], gv[:, kc, j * 128:(j + 1) * 128], hT[:, kc, :nt],
                           kc == 0, kc == C - 1, (gb, B_hT), (PS[bg],))
                    for kc in range(C):
                        mm(ps[:, bu, :nt], uv[:, kc, j * 128:(j + 1) * 128], hT[:, kc, :nt],
                           kc == 0, kc == C - 1, (ub, B_hT), (PS[bu],))
                    ACT(lambda e, f=f, bg=bg: e.activation(sgt[:, f % 2, :nt], ps[:, bg, :nt], AF.Silu),
                        (PS[bg],), (B_sgt[f % 2],))
                    DVE(lambda e, f=f, bu=bu: e.tensor_tensor(
                        aT[:, f, :nt], sgt[:, f % 2, :nt], ps[:, bu, :nt], ALU.mult),
                        (B_sgt[f % 2], PS[bu]), (B_aT[f],))
                    bg_some()
            for dp in range(8):
                banks = (4, 5) if dp % 2 == 0 else (6, 7)
                for half in range(2):
                    f0, f1 = (0, 22) if half == 0 else (22, NF)
                    dvw, db = ring_load(wdv[:, f0:f1, dp * 256:(dp + 1) * 256], f1 - f0, 256)
                    for dj in range(2):
                        for f in range(f0, f1):
                            mm(ps[:, banks[dj], :nt], dvw[:, f - f0, dj * 128:(dj + 1) * 128],
                               aT[:, f, :nt], f == 0, f == NF - 1, (db, B_aT[f]), (PS[banks[dj]],))
                for dj in range(2):
                    c = dp * 2 + dj
                    DVE(lambda e, c=c, bk=banks[dj], G=G: e.scalar_tensor_tensor(
                        xT[:, c, :nt], ps[:, bk, :nt], G[:, c:c + 1], xT[:, c, :nt], ALU.mult, ALU.add),
                        (PS[banks[dj]], B_xT[c], B_dv), (B_xT[c],))

        def load_x_input(tile):
            switch_big(FFN_BUFS)
            switch_B(B_xstage)
            src = xp if tile == 0 else xs
            r0 = 0 if tile == 0 else (tile - 1) * 512
            n = 0
            for sub in range(4):
                SPD(xstage[:, sub % 2, :], src[r0 + sub * 128: r0 + (sub + 1) * 128, :],
                    (), (B_xstage[sub % 2],))
                for c4 in range(4):
                    bank = 4 + n % 4
                    n += 1
                    for q in range(4):
                        c = c4 * 4 + q
                        PE(lambda e, bank=bank, q=q, c=c, sub=sub: e.transpose(
                            ps[:, bank, q * 128:(q + 1) * 128], xstage[:, sub % 2, c * 128:(c + 1) * 128],
                            ident[:]), (B_xstage[sub % 2], B_const), (PS[bank],))
                    eng = ACT if c4 % 2 == 0 else DVE
                    dst = xT[:, c4 * 4:(c4 + 1) * 4, sub * 128:(sub + 1) * 128]
                    srcp = ps[:, bank, :].rearrange("p (q t) -> p q t", t=128)
                    if c4 % 2 == 0:
                        ACT(lambda e, dst=dst, srcp=srcp: e.activation(dst, srcp, AF.Copy),
                            (PS[bank],), tuple(B_xT[c4 * 4:(c4 + 1) * 4]))
                    else:
                        DVE(lambda e, dst=dst, srcp=srcp: e.tensor_copy(dst, srcp),
                            (PS[bank],), tuple(B_xT[c4 * 4:(c4 + 1) * 4]))

        def store_x1(l, tile, nt):
            g0 = tile * 512
            SPD(x1T_s[l].rearrange("c p t -> p c t")[:, :, g0:g0 + nt], xT[:, :, :nt],
                tuple(B_xT), (B_x1T[l][tile],))

        def load_x1(l, tile, nt):
            g0 = tile * 512
            SPD(xT[:, :, :nt], x1T_s[l].rearrange("c p t -> p c t")[:, :, g0:g0 + nt],
                (B_x1T[l][tile],), tuple(B_xT))
            SPD(hT[:, :, :nt], uT_s[l].rearrange("c p t -> p c t")[:, :, g0:g0 + nt],
                (B_uT[l][tile],), (B_hT,))

        def in_proj1(nt, l, s, tile):
            is_p = tile == 0 and not st.get("noprompt", False)
            g0 = tile * 512
            rms_stats(nt)
            modulate(nt, l, s, 1)
            switch_big(IP_BUFS)
            SPD(uT_s[l].rearrange("c p t -> p c t")[:, :, g0:g0 + nt], hT[:, :, :nt],
                (B_hT,), (B_uT[l][tile],))
            wv = w_in[l].rearrange("(c p) n -> p c n", p=128)
            nst = [0]
            nsf = [0]
            npb = [0]

            def stb():
                i = nst[0] % 8
                nst[0] += 1
                return st_bf[:, i, :], B_stbf[i]

            def stf():
                i = nsf[0] % 4
                nsf[0] += 1
                return st_f32[:, i, :], B_stf[i]

            def pbank(lo, n):
                i = lo + npb[0] % n
                npb[0] += 1
                return i

            for half in range(2):
                av, ab = ring_load(wv[:, :, half * 512:(half + 1) * 512], C, 512)
                gvw, gbb = ring_load(wv[:, :, 1024 + half * 512:1024 + (half + 1) * 512], C, 512)
                for j in range(4):
                    ch = half * 4 + j
                    ba = (ch % 2) * 2
                    bgk = ba + 1
                    for kc in range(C):
                        mm(ps[:, ba, :nt], av[:, kc, j * 128:(j + 1) * 128], hT[:, kc, :nt],
                           kc == 0, kc == C - 1, (ab, B_hT), (PS[ba],))
                    for kc in range(C):
                        mm(ps[:, bgk, :nt], gvw[:, kc, j * 128:(j + 1) * 128], hT[:, kc, :nt],
                           kc == 0, kc == C - 1, (gbb, B_hT), (PS[bgk],))
                    sv, sbuf_ = stf()
                    ACT(lambda e, sv=sv, bgk=bgk: e.activation(sv[:, :nt], ps[:, bgk, :nt], AF.Sigmoid),
                        (PS[bgk],), (sbuf_,))
                    ov, obuf = stb()
                    DVE(lambda e, ov=ov, sv=sv, ba=ba: e.tensor_tensor(
                        ov[:, :nt], sv[:, :nt], ps[:, ba, :nt], ALU.mult), (sbuf_, PS[ba]), (obuf,))
                    SPD(hcT_s[l][ch, :, g0:g0 + nt], ov[:, :nt], (obuf,), (B_hcT[l][tile],))
            for which in range(2):
                dst_s = qT_s[l] if which == 0 else kT_s[l]
                dst_b = B_qT[l][tile] if which == 0 else B_kT[l][tile]
                for half in range(2):
                    cb0 = 2048 + which * 1024 + half * 512
                    bv, bb = ring_load(wv[:, :, cb0:cb0 + 512], C, 512)
                    for j in range(4):
                        ch = half * 4 + j
                        bk = pbank(0, 4)
                        for kc in range(C):
                            mm(ps[:, bk, :nt], bv[:, kc, j * 128:(j + 1) * 128], hT[:, kc, :nt],
                               kc == 0, kc == C - 1, (bb, B_hT), (PS[bk],))
                        ov, obuf = stb()
                        sc = 0.125 if which == 0 else 1.0
                        ACT(lambda e, ov=ov, bk=bk, sc=sc: e.activation(
                            ov[:, :nt], ps[:, bk, :nt], AF.Copy, scale=sc), (PS[bk],), (obuf,))
                        SPD(dst_s[ch, :, g0:g0 + nt], ov[:, :nt], (obuf,), (dst_b,))
                    if which == 1 and is_p:
                        for sub in range(nt // 128):
                            bk = pbank(4, 4)
                            for kc in range(C):
                                mm(ps[:, bk, :], hT[:, kc, sub * 128:(sub + 1) * 128], bv[:, kc, :],
                                   kc == 0, kc == C - 1, (bb, B_hT), (PS[bk],))
                            fv, fbuf = stf()
                            DVE(lambda e, fv=fv, bk=bk: e.tensor_copy(fv, ps[:, bk, :]), (PS[bk],), (fbuf,))
                            b_i, t0 = sub // 2, (sub % 2) * 128
                            if st.get("nk", True):
                                SPD(nk[b_i, l, t0:t0 + 128, half * 512:(half + 1) * 512], fv, (fbuf,), ())
            for half in range(2):
                cb0 = 4096 + half * 512
                bv, bb = ring_load(wv[:, :, cb0:cb0 + 512], C, 512)
                for sub in range(nt // 128):
                    bk = pbank(4, 4)
                    for kc in range(C):
                        mm(ps[:, bk, :], hT[:, kc, sub * 128:(sub + 1) * 128], bv[:, kc, :],
                           kc == 0, kc == C - 1, (bb, B_hT), (PS[bk],))
                    ov, obuf = stb()
                    ACT(lambda e, ov=ov, bk=bk: e.activation(ov, ps[:, bk, :], AF.Copy), (PS[bk],), (obuf,))
                    SPD(vt_s[l][g0 + sub * 128:g0 + (sub + 1) * 128, half * 512:(half + 1) * 512], ov,
                        (obuf,), (B_vt[l][tile],))
                    if is_p:
                        fv, fbuf = stf()
                        DVE(lambda e, fv=fv, bk=bk: e.tensor_copy(fv, ps[:, bk, :]), (PS[bk],), (fbuf,))
                        b_i, t0 = sub // 2, (sub % 2) * 128
                        if st.get("nv", True):
                            SPD(nv[b_i, l, t0:t0 + 128, half * 512:(half + 1) * 512], fv, (fbuf,), ())

        def conv_prepare(nt, l, tile, n_ffn):
            is_p = tile == 0
            g0 = tile * 512
            switch_B(CONV_BUFS)
            hsrc = hcT_s[l].rearrange("c p t -> p c t")
            hb3 = None
            if is_p:
                hb3 = hcbuf[:, :, 0:572].rearrange("p a (b t) -> p a b t", t=286)
                DVE(lambda e: e.memset(hcbuf[:, :, 0:572], 0.0), (), (B_hcbuf,))
                for b_i in range(2):
                    SPD(hb3[:, :, b_i, 15:271], hsrc[:, :, g0 + b_i * 256:g0 + (b_i + 1) * 256],
                        (B_hcT[l][tile],), (B_hcbuf,))
            else:
                lt0 = (tile - 1) * 512
                if lt0 == 0:
                    DVE(lambda e: e.memset(hcbuf[:, :, 0:15], 0.0), (), (B_hcbuf,))
                    SPD(hcbuf[:, :, 15:15 + nt + 15], hsrc[:, :, g0:g0 + nt + 15],
                        (B_hcT[l][tile], B_hcT[l][tile + 1]), (B_hcbuf,))
                else:
                    rd = [B_hcT[l][tile - 1], B_hcT[l][tile]]
                    if tile + 1 < NTOK // 512:
                        rd.append(B_hcT[l][tile + 1])
                    SPD(hcbuf[:, :, 0:nt + 30], hsrc[:, :, g0 - 15:g0 + nt + 15], tuple(rd), (B_hcbuf,))
            wset = 0 if is_p else 1
            cbv = convv[:, 0, l, :]
            conv_ops = []
            for chp in range(4):
                for jt in range(31):
                    for q in range(2):
                        ch = chp * 2 + q
                        wj = convw[:, wset, l, ch, jt:jt + 1]
                        if is_p:
                            acc = convacc[:, ch, :].rearrange("p (b t) -> p b t", t=256)
                            src = hb3[:, ch, :, jt:jt + 256]
                        else:
                            acc = convacc[:, ch, :nt]
                            src = hcbuf[:, ch, jt:jt + nt]
                        if jt == 0:
                            conv_ops.append(lambda acc=acc, src=src, wj=wj, ch=ch: DVE(
                                lambda e: e.tensor_scalar(acc, src, wj, cbv[:, ch:ch + 1], ALU.mult, ALU.add),
                                (B_hcbuf, B_const), (B_convacc[ch],)))
                        else:
                            conv_ops.append(lambda acc=acc, src=src, wj=wj, ch=ch: DVE(
                                lambda e: e.scalar_tensor_tensor(acc, src, wj, acc, ALU.mult, ALU.add),
                                (B_hcbuf, B_const, B_convacc[ch]), (B_convacc[ch],)))
            assert bg["it"] is None
            bg["it"] = iter(conv_ops)
            bg["k"] = -(-len(conv_ops) // (43 * max(n_ffn, 1))) if n_ffn > 0 else len(conv_ops)
            if n_ffn == 0:
                bg_some(len(conv_ops))

        def mixer_core(nt, l, tile):
            is_p = tile == 0
            g0 = tile * 512
            bg_some(1000)
            switch_big(B_convh + B_naoT + ATT_BUFS)
            lng = convv[:, 1, l, :]
            lnb = convv[:, 2, l, :]

            def conv_some(k):
                return

            conv_ops = []
            qsrc = qT_s[l]
            ksrc = kT_s[l]
            npair = nt // 128
            m0 = 0
            cm_lo = 0
            if is_p:
                kt0, nkc = g0, 4
                kbufs = [B_kT[l][0]]
                vbufs = [B_vt[l][0]]
            else:
                m0 = (tile - 1) * 4
                cm_lo = max(m0 - 2, 0)
                cm_hi = max(m0 + npair - 1 - 2, 0) + 4
                nkc = cm_hi - cm_lo + 1
                kt0 = 512 + cm_lo * 128
                tl = sorted(set((kt0 + i * 128) // 512 for i in range(nkc)))
                kbufs = [B_kT[l][t] for t in tl]
                vbufs = [B_vt[l][t] for t in tl]

            def loads(j):
                bj = B_kvq[j % 2]
                SPD(qb[:, j % 2, :nt], qsrc[j, :, g0:g0 + nt], (B_qT[l][tile],), (bj,))
                SPD(kb[:, j % 2, :nkc * 128], ksrc[j, :, kt0:kt0 + nkc * 128], tuple(kbufs), (bj,))
                SPD(vb[:, j % 2, :nkc, :],
                    vt_s[l][kt0:kt0 + nkc * 128, j * 128:(j + 1) * 128].rearrange("(c p) d -> p c d", p=128),
                    tuple(vbufs), (bj,))
                if not is_p:
                    SPD(ckb[:, j % 2, :], ckT_s[l][j], (B_ckT[l],), (bj,))
                    SPD(cvbuf[:, j % 2, :, :], cvb_s[l][j], (B_cvb[l],), (bj,))

            units = []
            for j in range(8):
                for hh in range(2):
                    if is_p:
                        for b_i in range(2):
                            for kk in range(2):
                                units.append(dict(kind="p", j=j, hh=hh, b_i=b_i, kk=kk, first=(b_i == 0 and kk == 0),
                                                  last=(b_i == 1 and kk == 1)))
                    else:
                        for cc in range(4):
                            units.append(dict(kind="c", j=j, hh=hh, cc=cc, first=(cc == 0), last=False))
                        for pi in range(npair):
                            units.append(dict(kind="l", j=j, hh=hh, pi=pi, first=False, last=(pi == npair - 1)))
            nu = len(units)
            per_unit = -(-len(conv_ops) // nu)
            state = dict(last_banks=set(), x1=0, x2=0, rot=0)

            def pick_banks(kind):
                if kind == "l":
                    cands = [(0, 1), (2, 3)]
                    if state["rot"] % 2:
                        cands = cands[::-1]
                else:
                    r = state["rot"] % 4
                    cands = [((r + i) % 4,) for i in range(4)]
                state["rot"] += 1
                for c_ in cands:
                    if not (set(c_) & state["last_banks"]):
                        state["last_banks"] = set(c_)
                        return c_
                state["last_banks"] = set(cands[0])
                return cands[0]

            def emit_S(u):
                j, hh = u["j"], u["hh"]
                bj = B_kvq[j % 2]
                jb = j % 2
                pr = slice(hh * 64, hh * 64 + 64)
                u["pr"] = pr
                if u["kind"] == "p":
                    (bk,) = pick_banks("p")
                    u["bk"] = bk
                    qs = slice(u["b_i"] * 256, (u["b_i"] + 1) * 256)
                    u["qs"] = qs
                    kc = u["b_i"] * 2 + u["kk"]
                    u["kc"] = kc
                    mm(ps[:, bk, 0:256], kb[pr, jb, kc * 128:(kc + 1) * 128], qb[pr, jb, qs],
                       True, True, (bj,), (PS[bk],))
                    x = state["x1"] % 2
                    state["x1"] += 1
                    u["x"] = x
                    ACT(lambda e, x=x, bk=bk: e.activation(pc[:, x, 0:256], ps[:, bk, 0:256], AF.Exp),
                        (PS[bk],), (B_pc[x],))
                elif u["kind"] == "c":
                    (bk,) = pick_banks("c")
                    cc = u["cc"]
                    mm(ps[:, bk, :nt], ckb[pr, jb, cc * 128:(cc + 1) * 128], qb[pr, jb, :nt],
                       True, True, (bj,), (PS[bk],))
                    x = state["x1"] % 2
                    state["x1"] += 1
                    u["x"] = x
                    ACT(lambda e, x=x, bk=bk: e.activation(pc[:, x, :nt], ps[:, bk, :nt], AF.Exp),
                        (PS[bk],), (B_pc[x],))
                else:
                    pb0, pb1 = pick_banks("l")
                    pi = u["pi"]
                    h = 2 * j + hh
                    m = m0 + pi
                    cm0 = max(m - 2, 0)
                    u["cm0"] = cm0
                    x = state["x2"] % 2
                    state["x2"] += 1
                    u["x"] = x
                    bsrc = bias_first[l, h, m] if m < 4 else bias_gen[l, h]
                    SPD(biasb[:, x, :], bsrc, (), (B_biasb[x],))
                    sreg = ps_flat[:, pb0 * 512: pb0 * 512 + 640]
                    qs = slice(pi * 128, (pi + 1) * 128)
                    u["qs"] = qs
                    for ci in range(5):
                        kc = cm0 + ci - cm_lo
                        bank_w = PS[pb0] if ci < 4 else PS[pb1]
                        mm(sreg[:, ci * 128:(ci + 1) * 128], kb[pr, jb, kc * 128:(kc + 1) * 128],
                           qb[pr, jb, qs], True, True, (bj,), (bank_w,))
                    DVE(lambda e, x=x, sreg=sreg: e.tensor_tensor(
                        sbb[:, x, 0:512], sreg[:, 0:512], biasb[:, x, 0:512], ALU.add),
                        (PS[pb0], B_biasb[x]), (B_sbb[x],))
                    DVE(lambda e, x=x, sreg=sreg: e.tensor_tensor(
                        sbb[:, x, 512:640], sreg[:, 512:640], biasb[:, x, 512:640], ALU.add),
                        (PS[pb1], B_biasb[x]), (B_sbb[x],))
                    ACT(lambda e, x=x: e.activation(pb[:, x, :], sbb[:, x, :], AF.Exp),
                        (B_sbb[x],), (B_pb[x],))
                conv_some(per_unit)

            def emit_PV(u):
                j, hh = u["j"], u["hh"]
                bj = B_kvq[j % 2]
                jb = j % 2
                bO = 4 + hh * 2
                bD = bO + 1
                x = u["x"]
                if u["kind"] == "p":
                    qs, kc, kk = u["qs"], u["kc"], u["kk"]
                    mm(ps[:, bO, qs], vb[:, jb, kc, :], pc[:, x, 0:256], kk == 0, kk == 1,
                       (bj, B_pc[x]), (PS[bO],))
                    mm(ps[:, bD, qs], ones_bf[:], pc[:, x, 0:256], kk == 0, kk == 1,
                       (B_pc[x], B_const), (PS[bD],))
                elif u["kind"] == "c":
                    cc = u["cc"]
                    mm(ps[:, bO, :nt], cvbuf[:, jb, cc, :], pc[:, x, :nt], cc == 0, False,
                       (bj, B_pc[x]), (PS[bO],))
                    mm(ps[:, bD, :nt], ones_bf[:], pc[:, x, :nt], cc == 0, False,
                       (B_pc[x], B_const), (PS[bD],))
                else:
                    qs, cm0 = u["qs"], u["cm0"]
                    for ci in range(5):
                        kc = cm0 + ci - cm_lo
                        lastc = ci == 4 and u["last"]
                        mm(ps[:, bO, qs], vb[:, jb, kc, :], pb[:, x, ci * 128:(ci + 1) * 128], False, lastc,
                           (bj, B_pb[x]), (PS[bO],))
                        mm(ps[:, bD, qs], ones_bf[:], pb[:, x, ci * 128:(ci + 1) * 128], False, lastc,
                           (B_pb[x], B_const), (PS[bD],))
                if u["last"]:
                    pr = u["pr"]
                    DVE(lambda e, pr=pr, bD=bD: e.reciprocal(rcp[pr, :nt], ps[pr, bD, :nt]), (PS[bD],), (B_rcp,))
                    DVE(lambda e, pr=pr, bO=bO, j=j: e.tensor_tensor(
                        naoT[pr, j, :nt], ps[pr, bO, :nt], rcp[pr, :nt], ALU.mult),
                        (PS[bO], B_rcp), (B_naoT[j],))
                    if hh == 1 and j + 2 < 8:
                        loads(j + 2)

            loads(0)
            loads(1)
            emit_S(units[0])
            for i, u in enumerate(units):
                if i + 1 < nu:
                    emit_S(units[i + 1])
                emit_PV(u)
            conv_some(len(conv_ops))

            for ch in range(8):
                ACT(lambda e, ch=ch: e.activation(sq[:, 0, :nt], convacc[:, ch, :nt], AF.Copy),
                    (B_convacc[ch],), (B_sq[0],))
                DVE(lambda e, ch=ch: e.tensor_tensor(
                    sq[:, 1, :nt], convacc[:, ch, :nt], convacc[:, ch, :nt], ALU.mult),
                    (B_convacc[ch],), (B_sq[1],))
                mm(ps[:, 0, :nt], ones_bf[:], sq[:, 0, :nt], ch == 0, ch == 7, (B_sq[0], B_const), (PS[0],))
                mm(ps[:, 1, :nt], ones_bf[:], sq[:, 1, :nt], ch == 0, ch == 7, (B_sq[1], B_const), (PS[1],))
            ACT(lambda e: e.activation(meant[:, :nt], ps[:, 0, :nt], AF.Copy, scale=1.0 / 1024),
                (PS[0],), (B_mean,))
            DVE(lambda e: e.tensor_tensor(tmpf[:, 0, :nt], meant[:, :nt], meant[:, :nt], ALU.mult),
                (B_mean,), (B_tmpf[0],))
            DVE(lambda e: e.scalar_tensor_tensor(tmpf[:, 1, :nt], ps[:, 1, :nt], 1.0 / 1024, tmpf[:, 0, :nt],
                                                 ALU.mult, ALU.subtract), (PS[1], B_tmpf[0]), (B_tmpf[1],))
            ACT(lambda e: e.activation(stdt[:, :nt], tmpf[:, 1, :nt], AF.Sqrt, bias=EPS), (B_tmpf[1],), (B_std,))
            DVE(lambda e: e.reciprocal(rstd[:, :nt], stdt[:, :nt]), (B_std,), (B_rstd,))
            for ch in range(8):
                DVE(lambda e, ch=ch: e.tensor_tensor(convacc[:, ch, :nt], convacc[:, ch, :nt], meant[:, :nt],
                                                     ALU.subtract), (B_convacc[ch], B_mean), (B_convacc[ch],))
                DVE(lambda e, ch=ch: e.tensor_tensor(convacc[:, ch, :nt], convacc[:, ch, :nt], rstd[:, :nt],
                                                     ALU.mult), (B_convacc[ch], B_rstd), (B_convacc[ch],))
                ACT(lambda e, ch=ch: e.activation(convh[:, ch, :nt], convacc[:, ch, :nt], AF.Silu,
                                                  bias=lnb[:, ch:ch + 1], scale=lng[:, ch:ch + 1]),
                    (B_convacc[ch], B_const), (B_convh[ch],))

        def merge_out(nt, l, s):
            switch_big(B_convh + B_naoT + MRG_BUFS)
            wv = w_in[l].rearrange("(c p) n -> p c n", p=128)
            cov = w_co[l].rearrange("(c p) n -> p c n", p=128)
            nov = w_no[l].rearrange("(c p) n -> p c n", p=128)
            wov = w_o[l].rearrange("(c p) n -> p c n", p=128)
            G = dv[:, l, s, 5, :]
            n = [0]
            for g4 in range(4):
                for br in range(2):
                    gcol = (5120 if br == 0 else 7168) + g4 * 512
                    gv, gb = ring_load(wv[:, :, gcol:gcol + 512], C, 512)
                    pv, pbuf = ring_load((cov if br == 0 else nov)[:, :, g4 * 512:(g4 + 1) * 512], 8, 512)
                    act_in = convh if br == 0 else naoT
                    act_b = B_convh if br == 0 else B_naoT
                    for jj in range(4):
                        c = g4 * 4 + jj
                        ba = (n[0] % 2) * 2
                        bb_ = ba + 1
                        n[0] += 1
                        for kc in range(C):
                            mm(ps[:, ba, :nt], gv[:, kc, jj * 128:(jj + 1) * 128], hT[:, kc, :nt],
                               kc == 0, kc == C - 1, (gb, B_hT), (PS[ba],))
                        for kc in range(8):
                            mm(ps[:, bb_, :nt], pv[:, kc, jj * 128:(jj + 1) * 128], act_in[:, kc, :nt],
                               kc == 0, kc == 7, (pbuf, act_b[kc]), (PS[bb_],))
                        x = n[0] % 2
                        ACT(lambda e, x=x, ba=ba: e.activation(tmpf[:, x, :nt], ps[:, ba, :nt], AF.Sigmoid),
                            (PS[ba],), (B_tmpf[x],))
                        if br == 0:
                            DVE(lambda e, x=x, bb_=bb_, jj=jj: e.tensor_tensor(
                                t1[:, jj, :nt], tmpf[:, x, :nt], ps[:, bb_, :nt], ALU.mult),
                                (B_tmpf[x], PS[bb_]), (B_t1[jj],))
                        else:
                            DVE(lambda e, x=x, bb_=bb_: e.tensor_tensor(
                                tmpf[:, x, :nt], tmpf[:, x, :nt], ps[:, bb_, :nt], ALU.mult),
                                (B_tmpf[x], PS[bb_]), (B_tmpf[x],))
                            DVE(lambda e, x=x, jj=jj, c=c: e.tensor_tensor(
                                mT[:, c, :nt], tmpf[:, x, :nt], t1[:, jj, :nt], ALU.add),
                                (B_tmpf[x], B_t1[jj]), (B_mT[c],))
            for g4 in range(4):
                ov, ob = ring_load(wov[:, :, g4 * 512:(g4 + 1) * 512], C, 512)
                for jj in range(4):
                    c = g4 * 4 + jj
                    bk = 4 + c % 4
                    for kc in range(C):
                        mm(ps[:, bk, :nt], ov[:, kc, jj * 128:(jj + 1) * 128], mT[:, kc, :nt],
                           kc == 0, kc == C - 1, (ob, B_mT[kc]), (PS[bk],))
                    DVE(lambda e, c=c, bk=bk: e.scalar_tensor_tensor(
                        xT[:, c, :nt], ps[:, bk, :nt], G[:, c:c + 1], xT[:, c, :nt], ALU.mult, ALU.add),
                        (PS[bk], B_xT[c], B_dv), (B_xT[c],))

        def final_out(nt, tile):
            rms_stats(nt)
            switch_big(B_ostage)
            for c in range(C):
                DVE(lambda e, c=c: e.scalar_tensor_tensor(
                    xT[:, c, :nt], xT[:, c, :nt], final_g[:, c:c + 1], rstd[:, :nt], ALU.mult, ALU.mult),
                    (B_xT[c], B_rstd, B_const), (B_xT[c],))
            dst = yp if tile == 0 else ys
            r0 = 0 if tile == 0 else (tile - 1) * 512
            n = 0
            for sub in range(nt // 128):
                for c4 in range(4):
                    bank = n % 4
                    n += 1
                    for q in range(4):
                        c = c4 * 4 + q
                        PE(lambda e, bank=bank, q=q, c=c, sub=sub: e.transpose(
                            ps[:, bank, q * 128:(q + 1) * 128], xT[:, c, sub * 128:(sub + 1) * 128], ident[:]),
                            (B_xT[c], B_const), (PS[bank],))
                    o_ap = ostage[:, sub % 2, c4 * 512:(c4 + 1) * 512]
                    if c4 % 2 == 0:
                        ACT(lambda e, o_ap=o_ap, bank=bank: e.activation(o_ap, ps[:, bank, :], AF.Copy),
                            (PS[bank],), (B_ostage[sub % 2],))
                    else:
                        DVE(lambda e, o_ap=o_ap, bank=bank: e.tensor_copy(o_ap, ps[:, bank, :]),
                            (PS[bank],), (B_ostage[sub % 2],))
                SPD(dst[r0 + sub * 128:r0 + (sub + 1) * 128, :], ostage[:, sub % 2, :], (B_ostage[sub % 2],), ())

        def sset(tile):
            return 0 if tile == 0 else 1

        st = stages or {}
        if dbg:
            dbg_dv = dout("dbg_dv", [128, DEPTH * 2 * 9 * C])
            SPD(dbg_dv, dv[:].rearrange("p a b c d -> p (a b c d)"), (B_dv,), ())
        parts = st.get("parts", "xfsi")
        mix_seq = [(0, t, 256 if t == 5 else 512) for t in st.get("p1", range(6))] + \
                  [(1, t, 512) for t in st.get("p2", range(5))]
        prepared = set()

        def prep_now(l, t, nt):
            if (l, t) not in prepared:
                conv_prepare(nt, l, t, 0)
                prepared.add((l, t))

        def prep_next(l, t, n_ffn):
            i = [m[:2] for m in mix_seq].index((l, t))
            if i + 1 < len(mix_seq):
                l2, t2, nt2 = mix_seq[i + 1]
                conv_prepare(nt2, l2, t2, n_ffn)
                prepared.add((l2, t2))
        for tile in st.get("p0", range(6)):
            if "x" in parts:
                load_x_input(tile)
            if tile == 5 and mix_seq and mix_seq[0][:2] == (0, 0):
                conv_prepare(512, 0, 0, 1)
                prepared.add((0, 0))
            if "f" in parts:
                ffn(512, 0, 0, sset(tile))
            if "s" in parts:
                store_x1(0, tile, 512)
            if "i" in parts:
                in_proj1(512, 0, sset(tile), tile)
        for tile in st.get("p1", range(6)):
            nt = 256 if tile == 5 else 512
            load_x1(0, tile, nt)
            prep_now(0, tile, nt)
            mixer_core(nt, 0, tile)
            merge_out(nt, 0, sset(tile))
            prep_next(0, tile, 2)
            ffn(nt, 0, 1, sset(tile))
            ffn(nt, 1, 0, sset(tile))
            store_x1(1, tile, nt)
            in_proj1(nt, 1, sset(tile), tile)
        for tile in st.get("p2", range(5)):
            load_x1(1, tile, 512)
            prep_now(1, tile, 512)
            mixer_core(512, 1, tile)
            merge_out(512, 1, sset(tile))
            prep_next(1, tile, 1)
            ffn(512, 1, 1, sset(tile))
            final_out(512, tile)

        emit_program(nc, P, esem, dsem, block)
    nc._k_declared = list(declared)
    return nc


def _fm(v):
    v = np.asarray(v, np.float32)
    lead = v.shape[:-1]
    n = v.shape[-1] // 128
    v = v.reshape(lead + (n, 128))
    return np.ascontiguousarray(np.moveaxis(v, -1, 0))


def _bias_tables(rel_bias, half):
    def table(m):
        cm0 = max(m - 2, 0)
        key = np.arange(128)
        qry = np.arange(128)
        ch = np.arange(5)
        rk_l = 2 * (cm0 + ch)[None, :, None] + (key // 64)[:, None, None]
        ck_l = (key % 64)[:, None, None] + 0 * ch[None, :, None]
        rq_l = (2 * m + qry // 64)[None, None, :]
        cq_l = (qry % 64)[None, None, :]
        if half == 0:
            rk, ckk, rq, cq = rk_l, ck_l, rq_l, cq_l
        else:
            rk, ckk, rq, cq = 63 - rk_l, 63 - ck_l, 63 - rq_l, 63 - cq_l
        rs = np.clip(rq - 4, 0, 56)
        cs = np.clip(cq - 8, 0, 48)
        valid = (rk >= rs) & (rk <= rs + 7) & (ckk >= cs) & (ckk <= cs + 15)
        dr = np.clip(rk - rq + 7, 0, 14)
        dc = np.clip(ckk - cq + 15, 0, 30)
        dr, dc, valid = np.broadcast_arrays(dr, dc, valid)
        g = rel_bias[:, :, dr, dc]
        g = np.where(valid[None, None], g, np.float32(NEG)).astype(np.float32)
        return g.reshape(DEPTH, NH, 128, 640)
    first = np.stack([table(m) for m in range(4)], axis=2)
    gen = table(8)
    return np.ascontiguousarray(first), np.ascontiguousarray(gen)


_NC_CACHE = {}


def _get_nc(dbg=False):
    if dbg not in _NC_CACHE:
        _NC_CACHE[dbg] = build_program(dbg)
    return _NC_CACHE[dbg]


def make_in_maps(x_prompt, x_sample, cache_k, cache_v, c, c_ctx, w_ada, b_ada, norm_g,
                 ffn_w_gate, ffn_w_up, ffn_w_down, w_in, conv_w, conv_b, conv_ln_g,
                 conv_ln_b, w_conv_out, rel_bias, w_na_out, w_out, final_g):
    f = lambda a: np.ascontiguousarray(np.asarray(a, np.float32))
    x_prompt, x_sample, cache_k, cache_v = f(x_prompt), f(x_sample), f(cache_k), f(cache_v)
    c, c_ctx, rel_bias, conv_w = f(c), f(c_ctx), f(rel_bias), f(conv_w)
    shared = {
        "w_ada": f(w_ada), "wg": f(ffn_w_gate), "wu": f(ffn_w_up), "wd": f(ffn_w_down),
        "w_in": f(w_in), "w_co": f(w_conv_out), "w_no": f(w_na_out), "w_o": f(w_out),
        "b_ada_t": _fm(b_ada), "norm_g_t": _fm(norm_g), "final_g_t": _fm(final_g),
        "convv_t": _fm(np.stack([f(conv_b), f(conv_ln_g), f(conv_ln_b)], 0)),
        "ident": np.eye(128, dtype=np.float32),
    }
    def taps(cw):
        t = np.transpose(cw, (0, 2, 1)).reshape(DEPTH, 8, 128, 31)
        return np.ascontiguousarray(np.transpose(t, (2, 0, 1, 3)))
    tp_nat = taps(conv_w)
    tp_rev = taps(conv_w[:, ::-1, :])
    tables = [_bias_tables(rel_bias, 0), _bias_tables(rel_bias, 1)]
    in_maps = []
    for i in range(8):
        b, half = i // 2, i % 2
        if half == 0:
            xs = x_sample[b, 0:2560]
            cws = tp_nat
        else:
            xs = x_sample[b, ::-1][0:2560]
            cws = tp_rev
        m = dict(shared)
        m["xp"] = np.ascontiguousarray(x_prompt[2 * i:2 * i + 2].reshape(512, D))
        m["xs"] = np.ascontiguousarray(xs)
        m["ck"] = np.ascontiguousarray(cache_k[b].reshape(DEPTH, 512, 1024))
        m["cv"] = np.ascontiguousarray(cache_v[b].reshape(DEPTH, 512, 1024))
        m["cvec"] = np.ascontiguousarray(np.stack([_fm(c_ctx), _fm(c[b])], axis=-1))
        m["convw_t"] = np.ascontiguousarray(np.stack([tp_nat, cws], axis=1))
        m["bias_first"], m["bias_gen"] = tables[half]
        in_maps.append(m)
    return in_maps


def assemble(results):
    y_p = np.zeros((16, 256, D), np.float32)
    y_s = np.zeros((4, 4096, D), np.float32)
    n_k = np.zeros((16, DEPTH, 256, NH, 64), np.float32)
    n_v = np.zeros((16, DEPTH, 256, NH, 64), np.float32)
    for i, r in enumerate(results):
        b, half = i // 2, i % 2
        y_p[2 * i:2 * i + 2] = np.asarray(r["yp"]).reshape(2, 256, D)
        ysl = np.asarray(r["ys"])
        if half == 0:
            y_s[b, 0:2048] = ysl
        else:
            y_s[b, 2048:4096] = ysl[::-1]
        n_k[2 * i:2 * i + 2] = np.asarray(r["nk"]).reshape(2, DEPTH, 256, NH, 64)
        n_v[2 * i:2 * i + 2] = np.asarray(r["nv"]).reshape(2, DEPTH, 256, NH, 64)
    return y_p, y_s, n_k, n_v


def kernel(**inputs):
    nc = _get_nc(False)
    in_maps = make_in_maps(**inputs)
    names = set(nc._k_declared)
    in_maps = [{k: v for k, v in m.items() if k in names} for m in in_maps]
    res = run_bass_kernel_spmd(nc, in_maps, core_ids=list(range(8)))
    return assemble(res.results)
```
